# Optimizing a Trainium2 kernel written in Bass

```python
import jax, jax.numpy as jnp
from jax import lax
import numpy as np

D_MODEL = 2048
BATCH = 4
SEQ = 8192
DEPTH = 2

CHUNK = 64
N_META = 16
N_MIXERS = 2
D_FF = 5632
POOL_WINDOWS = (2, 4, 8, 16)
N_POOL_GROUPS = len(POOL_WINDOWS)
POOL_GROUP = D_MODEL // N_POOL_GROUPS
SB_HEAD_DIM = 128
SB_HEADS = D_MODEL // SB_HEAD_DIM
Q_BLOCK = 128
N_POOL_LAYERS = (DEPTH + 1) // 2
N_SB_LAYERS = DEPTH // 2
RMS_EPS = 1e-6

kernel_name = 'interleaved_pool_stickbreak_macaron'


def rms_norm(x, g):
    xf = x.astype(jnp.float32)
    y = xf * lax.rsqrt(jnp.mean(xf * xf, axis=-1, keepdims=True) + RMS_EPS)
    return (y * g.astype(jnp.float32)).astype(x.dtype)


def half_swiglu(x, g, w_in, w_out):
    h = rms_norm(x, g)
    gate, up = jnp.split(h @ w_in, 2, axis=-1)
    return x + 0.5 * ((jax.nn.silu(gate) * up) @ w_out)


def pool_mixer(x, g, w_pool, scale):
    T = x.shape[1]
    hf = rms_norm(x, g).astype(jnp.float32)
    count_pos = jnp.arange(T) + 1
    outs = []
    for i, w in enumerate(POOL_WINDOWS):
        hg = hf[..., i * POOL_GROUP:(i + 1) * POOL_GROUP]
        cs = jnp.cumsum(hg, axis=1)
        lag = jnp.pad(cs, ((0, 0), (w, 0), (0, 0)))[:, :T]
        cnt = jnp.minimum(count_pos, w).astype(jnp.float32)[None, :, None]
        y = (cs - lag) / cnt - hg
        outs.append(jnp.einsum('btc,cd->btd', y.astype(x.dtype), w_pool[i]))
    return x + scale * jnp.concatenate(outs, axis=-1)


def stick_breaking_mixer(x, g, w_qkv, qk_norm, w_o):
    B, T, _ = x.shape
    h = rms_norm(x, g)
    q, k, v = jnp.split(h @ w_qkv, 3, axis=-1)
    q = rms_norm(q.reshape(B, T, SB_HEADS, SB_HEAD_DIM), qk_norm[0])
    k = rms_norm(k.reshape(B, T, SB_HEADS, SB_HEAD_DIM), qk_norm[1])
    v = v.reshape(B, T, SB_HEADS, SB_HEAD_DIM)
    n_blocks = -(-T // Q_BLOCK)
    Tp = n_blocks * Q_BLOCK
    pad = ((0, 0), (0, Tp - T), (0, 0), (0, 0))
    q, k, v = [jnp.pad(a, pad).transpose(0, 2, 1, 3) for a in (q, k, v)]
    qf = q.astype(jnp.float32) * (SB_HEAD_DIM ** -0.5)
    kf = k.astype(jnp.float32)
    key_pos = jnp.arange(Tp)

    def block(i):
        start = i * Q_BLOCK
        qb = lax.dynamic_slice_in_dim(qf, start, Q_BLOCK, axis=2)
        z = jnp.einsum('bhqd,bhkd->bhqk', qb, kf)
        q_pos = start + jnp.arange(Q_BLOCK)
        mask = key_pos[None, :] < q_pos[:, None]
        log_beta = jax.nn.log_sigmoid(z)
        log_stay = jnp.where(mask, log_beta - z, 0.0)
        after = lax.cumsum(log_stay, axis=3, reverse=True) - log_stay
        a = jnp.where(mask, jnp.exp(log_beta + after), 0.0)
        return jnp.einsum('bhqk,bhkd->bhqd', a.astype(v.dtype), v)

    o = lax.map(block, jnp.arange(n_blocks))
    o = o.transpose(1, 0, 3, 2, 4).reshape(B, Tp, D_MODEL)[:, :T]
    return x + o @ w_o


def setup_inputs(seed: int = 0) -> dict:
    key = jax.random.key(seed)
    ks = jax.random.split(key, 12)
    f32 = jnp.float32
    x = jax.random.normal(ks[0], (BATCH, SEQ, D_MODEL), f32)
    meta = jax.random.normal(ks[1], (N_META, D_MODEL), f32)
    ffn_norm = 1.0 + 0.02 * jax.random.normal(ks[2], (DEPTH, 2, D_MODEL), f32)
    ffn_w_in = jax.random.normal(ks[3], (DEPTH, 2, D_MODEL, 2 * D_FF), f32) * D_MODEL ** -0.5
    ffn_w_out = jax.random.normal(ks[4], (DEPTH, 2, D_FF, D_MODEL), f32) * D_FF ** -0.5
    mix_norm = 1.0 + 0.02 * jax.random.normal(ks[5], (DEPTH, D_MODEL), f32)
    pool_w = jax.random.normal(ks[6], (N_POOL_LAYERS, N_POOL_GROUPS, POOL_GROUP, POOL_GROUP), f32) * POOL_GROUP ** -0.5
    pool_scale = 1.0 + 0.02 * jax.random.normal(ks[7], (N_POOL_LAYERS, D_MODEL), f32)
    sb_w_qkv = jax.random.normal(ks[8], (N_SB_LAYERS, D_MODEL, 3 * D_MODEL), f32) * D_MODEL ** -0.5
    sb_qk_norm = 1.0 + 0.02 * jax.random.normal(ks[9], (N_SB_LAYERS, 2, SB_HEAD_DIM), f32)
    sb_w_o = jax.random.normal(ks[10], (N_SB_LAYERS, D_MODEL, D_MODEL), f32) * D_MODEL ** -0.5
    return {'x': x, 'meta': meta, 'ffn_norm': ffn_norm, 'ffn_w_in': ffn_w_in, 'ffn_w_out': ffn_w_out,
            'mix_norm': mix_norm, 'pool_w': pool_w, 'pool_scale': pool_scale, 'sb_w_qkv': sb_w_qkv,
            'sb_qk_norm': sb_qk_norm, 'sb_w_o': sb_w_o}


def reference(x, meta, ffn_norm, ffn_w_in, ffn_w_out, mix_norm, pool_w, pool_scale, sb_w_qkv, sb_qk_norm, sb_w_o):
    B = x.shape[0]
    meta_b = jnp.broadcast_to(meta.astype(x.dtype)[None], (B, N_META, D_MODEL))
    h = jnp.concatenate([meta_b, x], axis=1)
    for layer in range(DEPTH):
        h = half_swiglu(h, ffn_norm[layer, 0], ffn_w_in[layer, 0], ffn_w_out[layer, 0])
        j = layer // N_MIXERS
        if layer % N_MIXERS == 0:
            h = pool_mixer(h, mix_norm[layer], pool_w[j], pool_scale[j])
        else:
            h = stick_breaking_mixer(h, mix_norm[layer], sb_w_qkv[j], sb_qk_norm[j], sb_w_o[j])
        h = half_swiglu(h, ffn_norm[layer, 1], ffn_w_in[layer, 1], ffn_w_out[layer, 1])
    return h[:, N_META:]
```

```python
import contextlib
import numpy as np
import ml_dtypes
import concourse.bass as bass
import concourse.mybir as mybir
from concourse.bass_utils import run_bass_kernel_spmd

F32 = mybir.dt.float32
BF16 = mybir.dt.bfloat16
AF = mybir.ActivationFunctionType
ALU = mybir.AluOpType

D = 2048
DC = 16
FF = 5632
FC = 44
FA = 11
ARC = 44
T = 456
HALO = 16
W = T + HALO
NT = 9
KB = 114
NKB = 4
NCH = 18
SEQ = 8192
NMETA = 16
EPS = 1e-6
NSLOT = 3
SAME_ENGINE_SYNC = False
FUSED = True

G_FFN = 0
G_MIX = 64
G_PSC = 96
G_QN = 112
G_KN = 113
NG = 114


class Hd:
    __slots__ = ("eng", "idx", "key", "ord", "sig", "unit")

    def __init__(self, eng, idx, key=None, ordn=0, unit=16):
        self.eng, self.idx, self.key, self.ord, self.sig, self.unit = eng, idx, key, ordn, 0, unit


class Res:
    def __init__(self):
        self.w = None
        self.r = []

    def rdeps(self):
        return [self.w] if self.w is not None else []

    def wdeps(self):
        return ([self.w] if self.w is not None else []) + self.r


class Sched:
    ENGS = ("sp", "pe", "act", "dve", "pool")

    def __init__(self):
        self.ops = {e: [] for e in self.ENGS}
        self.dma_count = {}

    def op(self, eng, fn, reads=(), writes=(), extra=(), key=None, wait=True, track=True, unit=16):
        deps = list(extra)
        if wait:
            for r in reads:
                deps += r.rdeps()
            for w in writes:
                deps += w.wdeps()
        if key is not None:
            self.dma_count[key] = self.dma_count.get(key, 0) + 1
            h = Hd(eng, len(self.ops[eng]), key, self.dma_count[key], unit)
        else:
            h = Hd(eng, len(self.ops[eng]))
        self.ops[eng].append((fn, deps, h))
        if track:
            for r in reads:
                r.r.append(h)
            for w in writes:
                w.w = h
                w.r = []
        return h

    def emit(self, nc, block, sems, dsems):
        need = set()
        for e in self.ENGS:
            for fn, deps, h in self.ops[e]:
                for d in deps:
                    if d.key is None:
                        if d.eng == e and (e == "pe" or not SAME_ENGINE_SYNC):
                            continue
                        need.add(id(d))
        for e in self.ENGS:
            cnt = 0
            for fn, deps, h in self.ops[e]:
                if h.key is None and id(h) in need:
                    cnt += 1
                    h.sig = cnt

        def run(e, engobj):
            waited = {}
            for fn, deps, h in self.ops[e]:
                for d in deps:
                    if d.key is not None:
                        sk, val, sem = ("d", d.key), d.unit * d.ord, dsems[d.key]
                    else:
                        if d.eng == e and (e == "pe" or not SAME_ENGINE_SYNC):
                            continue
                        sk, val, sem = ("e", d.eng), d.sig, sems[d.eng]
                    if waited.get(sk, 0) >= val:
                        continue
                    engobj.wait_ge(sem, val)
                    waited[sk] = val
                ins = fn(engobj)
                if h.key is not None:
                    ins.then_inc(dsems[h.key], h.unit)
                elif h.sig:
                    ins.then_inc(sems[e], 1)

        @block.sync
        def _(eng):
            run("sp", eng)

        @block.tensor
        def _(eng):
            run("pe", eng)

        @block.scalar
        def _(eng):
            run("act", eng)

        @block.vector
        def _(eng):
            run("dve", eng)

        @block.gpsimd
        def _(eng):
            run("pool", eng)


def build_program(mode, nseq=4):
    nc = bass.Bass("TRN2", target_bir_lowering=False)
    doA = "A" in mode
    doB = "B" in mode
    fused = mode == "AB"

    def dram(name, shape, dt, kind):
        return nc.dram_tensor(name, list(shape), dt, kind=kind).ap()

    IN, OUT, INT = "ExternalInput", "ExternalOutput", "Internal"
    w_in = dram("ffn_w_in", [2, 2, D, 2 * FF], F32, IN)
    w_out = dram("ffn_w_out", [2, 2, FF, D], F32, IN)
    gains_d = dram("gains", [128, NG], F32, IN)
    ident_d = dram("ident", [128, 128], F32, IN)
    cbf_d = dram("cbf", [128, 4, 128], BF16, IN)
    if doA:
        xin = dram("xin", [NT, W, D], F32, IN)
        invc_d = dram("invc", [2, 128, 4, T], F32, IN)
        pool_w = dram("pool_w", [4, 512, 512], F32, IN)
        w_qkv = dram("sb_w_qkv", [D, 3 * D], F32, IN)
    if doB:
        w_o = dram("sb_w_o", [D, D], F32, IN)
        mask_d = dram("mask", [KB, 2 * NKB, T], BF16, IN)
        yout = dram("y", [NT, T, D], F32, OUT)
    KW = DC * T
    VW = DC * NKB * 128
    if fused:
        h1s_w = nc.dram_tensor("h1s", [NT, 128, KW], F32).ap()
        qs_w = nc.dram_tensor("qs", [NT, 128, KW], BF16).ap()
        kmine_ts = [nc.dram_tensor(f"kmine{i}", [128, KW], BF16) for i in range(NT)]
        vmine_ts = [nc.dram_tensor(f"vmine{i}", [KB, VW], BF16) for i in range(NT)]
        kbuf_ts = [nc.dram_tensor(f"kbuf{i}", [2 * 128, KW], BF16) for i in range(NT)]
        vbuf_ts = [nc.dram_tensor(f"vbuf{i}", [2 * KB, VW], BF16) for i in range(NT)]
        h1s_r, qs_r = h1s_w, qs_w
    else:
        if doA:
            h1s_w = dram("h1s", [NT, 128, KW], F32, OUT)
            qs_w = dram("qs", [NT, 128, KW], BF16, OUT)
            kmine = dram("kmine", [NT * 128, KW], BF16, OUT)
            vmine = dram("vmine", [NT * KB, VW], BF16, OUT)
        if doB:
            h1s_r = dram("h1s", [NT, 128, KW], F32, IN)
            qs_r = dram("qs", [NT, 128, KW], BF16, IN)
            kbuf = dram("kbuf", [2 * NT * 128, KW], BF16, IN)
            vbuf = dram("vbuf", [2 * NT * KB, VW], BF16, IN)

    def kmine_ap(i):
        return kmine_ts[i].ap() if fused else kmine[i * 128:(i + 1) * 128, :]

    def vmine_ap(i):
        return vmine_ts[i].ap() if fused else vmine[i * KB:(i + 1) * KB, :]

    def kbuf_ap(jj):
        if fused:
            return kbuf_ts[jj // 2].ap()[(jj % 2) * 128:(jj % 2 + 1) * 128, :]
        r = (jj % 2) * NT + jj // 2
        return kbuf[r * 128:(r + 1) * 128, :]

    def vbuf_ap(jj):
        if fused:
            return vbuf_ts[jj // 2].ap()[(jj % 2) * KB:(jj % 2 + 1) * KB, :]
        r = (jj % 2) * NT + jj // 2
        return vbuf[r * KB:(r + 1) * KB, :]

    tiles = []

    def c_p(ap2d):
        return ap2d.rearrange("(c p) n -> p c n", p=128)

    def add_win(l, j):
        src = c_p(w_in[l, j])
        for ftp in range(FC // 2):
            halves = []
            for hh in range(2):
                parts = []
                for gu in range(2):
                    s = src[:, 8 * hh:8 * hh + 8, gu * FF + 256 * ftp: gu * FF + 256 * ftp + 256]
                    parts.append((("win", gu), s))
                halves.append((hh * 4096, 4096, parts))
            tiles.append((("win", l, j, ftp), halves))

    def add_wout(l, j):
        src = c_p(w_out[l, j])
        for dg in range(4):
            for fq in range(4):
                halves = []
                h0 = (FA + 1) // 2
                for (a, n) in (((0, h0), (h0, FA - h0)) if FA > 1 else ((0, 1),)):
                    s = src[:, fq * FA + a: fq * FA + a + n, dg * 512:(dg + 1) * 512]
                    halves.append((a * 512, n * 512, [(("flat", n, 512), s)]))
                tiles.append((("wout", l, j, dg, fq), halves))

    def add_cp(name, ap2d, col0):
        src = c_p(ap2d)
        halves = []
        for hh in range(2):
            s = src[:, 8 * hh:8 * hh + 8, col0:col0 + 512]
            halves.append((hh * 4096, 4096, [(("flat", 8, 512), s)]))
        tiles.append((name, halves))

    if doA:
        add_win(0, 0); add_wout(0, 0)
        src = pool_w.rearrange("g (kc p) d -> p g kc d", p=128)
        halves = []
        for hh in range(2):
            halves.append((hh * 4096, 4096, [(("pool",), src[:, 2 * hh:2 * hh + 2, :, :])]))
        tiles.append((("pool",), halves))
        add_win(0, 1); add_wout(0, 1)
        add_win(1, 0); add_wout(1, 0)
        for hq in range(4):
            add_cp(("q", hq), w_qkv, hq * 512)
        for hq in range(4):
            add_cp(("k", hq), w_qkv, D + hq * 512)
        for hq in range(4):
            add_cp(("v", hq), w_qkv, 2 * D + hq * 512)
    if doB:
        for dg in range(4):
            add_cp(("o", dg), w_o, dg * 512)
        add_win(1, 1); add_wout(1, 1)
    tile_idx = {name: i for i, (name, _) in enumerate(tiles)}
    NWT = len(tiles)
    WPART = 64
    wsc_parts = [dram(f"wsc{q_}", [min(WPART, NWT - q_ * WPART), 128, 8192], BF16, INT)
                 for q_ in range((NWT + WPART - 1) // WPART)]

    def wsc_t(ti):
        return wsc_parts[ti // WPART][ti % WPART]

    S = Sched()
    es = contextlib.ExitStack()

    def sb(name, shape, dt):
        return es.enter_context(nc.sbuf_tensor(name, list(shape), dt))

    with es:
        R = sb("R", [128, DC, W], F32)
        HN = sb("HN", [128, DC, W], BF16)
        ACTb = sb("ACTb", [128, ARC, W], BF16)
        RING = [sb(f"ring{i}", [128, 8192], BF16) for i in range(NSLOT)]
        GN = sb("GN", [128, NG], F32)
        IDN = sb("IDN", [128, 128], F32)
        CBF = sb("CBF", [128, 4, 128], BF16)
        RS = [sb(f"RS{i}", [128, W], F32) for i in range(2)]
        SG = [sb(f"SG{i}", [128, W], F32) for i in range(2)]
        if doA:
            WA = sb("WA", [128, 4, W], F32)
            WB = sb("WB", [128, 4, W], F32)
            INVC = sb("INVC", [128, 2, 4, T], F32)
        if doB:
            MASK = sb("MASK", [KB, 2 * NKB, T], BF16)
            KTb = [sb(f"KT{i}", [128, T], BF16) for i in range(4)]
            Vb = [sb(f"V{i}", [KB, NKB, 128], BF16) for i in range(4)]
            Eb = [sb(f"E{i}", [KB, T], F32) for i in range(3)]
            SPb = [sb(f"SP{i}", [KB, T], BF16) for i in range(3)]
            Sb = [sb(f"S{i}", [KB, T], BF16) for i in range(3)]
            EXb = [sb(f"EX{i}", [KB, T], BF16) for i in range(3)]
            Ab = [sb(f"A{i}", [KB, T], BF16) for i in range(4)]
        BANK = [es.enter_context(nc.psum_tensor(f"bank{i}", [128, 512], F32)) for i in range(8)]


        rR = [Res() for _ in range(DC)]
        rHN = [Res() for _ in range(DC)]
        rACT = [Res() for _ in range(ARC)]
        rBANK = [Res() for _ in range(8)]
        rSLOT = [Res() for _ in range(NSLOT)]
        rRS = [Res() for _ in range(2)]
        rSG = [Res() for _ in range(2)]
        rCONST = Res()
        rWA, rWB = Res(), Res()
        rTILE = [Res() for _ in range(NWT)]

        def flat(t):
            return t[:, :, :].rearrange("p a b -> p (a b)")

        ACTflat = flat(ACTb)
        ACTf32 = ACTflat.bitcast(F32)
        HNflat = flat(HN)

        S.op("sp", lambda e: e.dma_start(out=GN[:, :], in_=gains_d[:, :]), writes=[rCONST], key="c0")
        S.op("sp", lambda e: e.dma_start(out=IDN[:, :], in_=ident_d[:, :]), writes=[rCONST], key="c0")
        S.op("sp", lambda e: e.dma_start(out=CBF[:, :, :], in_=cbf_d[:, :, :]), writes=[rCONST], key="c0")
        if doA:
            S.op("sp", lambda e: e.dma_start(out=INVC[:, :, :, :], in_=invc_d.rearrange("v p g t -> p v g t")),
                 writes=[rCONST], key="c0")
        if doB:
            S.op("sp", lambda e: e.dma_start(out=MASK[:, :, :], in_=mask_d[:, :, :]), writes=[rCONST], key="c0")
        const_deps = [S.ops["sp"][-1][2]]

        S32 = [ACTf32[:, 0:4096], ACTf32[:, 4096:8192]]
        S16 = [ACTflat[:, 16384:20480], HNflat[:, 0:4096]]
        rS32 = [Res(), Res()]
        rS16 = [Res(), Res()]
        k = 0
        for ti, (name, halves) in enumerate(tiles):
            for (off, nel, parts) in halves:
                hb = k % 2
                for (kind, src) in parts:
                    if kind[0] == "win":
                        gu = kind[1]
                        dst = S32[hb].rearrange("p (c g f) -> p c g f", c=8, g=2, f=256)[:, :, gu, :]
                    elif kind[0] == "flat":
                        dst = S32[hb][:, 0:kind[1] * kind[2]].rearrange("p (a b) -> p a b", a=kind[1])
                    else:
                        dst = S32[hb].rearrange("p (g kc d) -> p g kc d", g=2, kc=4, d=512)
                    S.op("sp", (lambda e, dst=dst, src=src: e.dma_start(out=dst, in_=src)),
                         writes=[rS32[hb]], key=f"cl{hb}")
                src32 = S32[hb][:, 0:nel]
                dst16 = S16[hb][:, 0:nel]
                if k % 2 == 0:
                    S.op("dve", (lambda e, a=dst16, b=src32: e.tensor_copy(a, b)),
                         reads=[rS32[hb]], writes=[rS16[hb]])
                else:
                    S.op("act", (lambda e, a=dst16, b=src32: e.activation(a, b, AF.Copy)),
                         reads=[rS32[hb]], writes=[rS16[hb]])
                S.op("sp", (lambda e, a=wsc_t(ti)[:, off:off + nel], b=dst16: e.dma_start(out=a, in_=b)),
                     reads=[rS16[hb]], writes=[], key=f"cs{hb}")
                rTILE[ti].r.append(S.ops["sp"][-1][2])
                k += 1
        conv_fence = []
        for r in rS32 + rS16:
            conv_fence += r.wdeps()
        for r in rACT + rHN:
            r.r = list(conv_fence)

        wstate = {"n": 0}

        def wload(name):
            ti = tile_idx[name]
            s = wstate["n"] % NSLOT
            wstate["n"] += 1
            nv = sum(h[1] for h in tiles[ti][1])
            S.op("sp", (lambda e, s=s, ti=ti, nv=nv: e.dma_start(out=RING[s][:, 0:nv], in_=wsc_t(ti)[:, 0:nv])),
                 writes=[rSLOT[s]], extra=list(rTILE[ti].r), key=f"wl{s}")
            return s

        bank_rr = {"n": 0}

        def nbank():
            b = bank_rr["n"] % 8
            bank_rr["n"] += 1
            return b

        def mm_group(b, parts, n, mms, reads):
            nm = len(mms)
            for i, (l, r) in enumerate(mms):
                out = BANK[b][0:parts, 0:n]
                fn = (lambda e, out=out, l=l, r=r, i=i: e.matmul(out, l, r, start=(i == 0), stop=(i == nm - 1)))
                if i == 0 or i == nm - 1:
                    S.op("pe", fn, reads=reads, writes=[rBANK[b]], wait=(i == 0), track=(i == nm - 1))
                else:
                    S.op("pe", fn)

        import os
        STOP = int(os.environ.get("KSTOP", "99"))

        def rms_stats(c0, n, scale_ones, nch, src_fn, rs_i, sq_scale=1.0, sq_bias=EPS):
            for c in range(nch):
                S.op("pool", (lambda e, c=c: e.tensor_tensor(ACTb[:, c, 0:n], src_fn(c), src_fn(c), ALU.mult)),
                     reads=[rR[c]], writes=[rACT[c]])
            b = nbank()
            mm_group(b, 128, n, [(CBF[:, scale_ones, :], ACTb[:, c, 0:n]) for c in range(nch)],
                     reads=[rACT[c] for c in range(nch)])
            S.op("act", (lambda e: e.activation(RS[rs_i][:, 0:n], BANK[b][:, 0:n], AF.Sqrt, bias=sq_bias, scale=sq_scale)),
                 reads=[rBANK[b]], writes=[rRS[rs_i]])
            S.op("dve", (lambda e: e.reciprocal(RS[rs_i][:, 0:n], RS[rs_i][:, 0:n])),
                 reads=[rRS[rs_i]], writes=[rRS[rs_i]])

        def norm_to_hn(c0, n, gcol, rs_i):
            for c in range(DC):
                S.op("dve", (lambda e, c=c: e.scalar_tensor_tensor(
                    HN[:, c, 0:n], R[:, c, c0:c0 + n], GN[:, gcol + c:gcol + c + 1], RS[rs_i][:, 0:n],
                    ALU.mult, ALU.mult)), reads=[rR[c], rRS[rs_i]], writes=[rHN[c]], extra=const_deps)

        def ffn(l, j, c0, n):
            rms_stats(c0, n, 0, DC, lambda c: R[:, c, c0:c0 + n], 0)
            norm_to_hn(c0, n, G_FFN + (l * 2 + j) * 16, 0)
            sgi = 0
            for ftp in range(FC // 2):
                s = wload(("win", l, j, ftp))
                wt = RING[s][:, :].rearrange("p (c g t f) -> p c g t f", c=16, g=2, t=2, f=128)
                for ft in range(2):
                    f = 2 * ftp + ft
                    bg, bu = nbank(), nbank()
                    for (bb, gu) in ((bg, 0), (bu, 1)):
                        mm_group(bb, 128, n, [(wt[:, c, gu, ft, :], HN[:, c, 0:n]) for c in range(DC)],
                                 reads=[rSLOT[s]] + rHN)
                    q = sgi % 2
                    sgi += 1
                    S.op("act", (lambda e, q=q, bg=bg: e.activation(SG[q][:, 0:n], BANK[bg][:, 0:n], AF.Silu)),
                         reads=[rBANK[bg]], writes=[rSG[q]])
                    S.op("dve", (lambda e, q=q, bu=bu, f=f: e.tensor_tensor(
                        ACTb[:, f, 0:n], SG[q][:, 0:n], BANK[bu][:, 0:n], ALU.mult)),
                        reads=[rSG[q], rBANK[bu]], writes=[rACT[f]])
            for dg in range(4):
                bs = [nbank() for _ in range(4)]
                for fq in range(4):
                    s = wload(("wout", l, j, dg, fq))
                    wt = RING[s][:, 0:FA * 512].rearrange("p (a d) -> p a d", a=FA)
                    for a in range(FA):
                        fc = fq * FA + a
                        for dc in range(4):
                            first = (fq == 0 and a == 0)
                            last = (fq == 3 and a == FA - 1)
                            lastslot = (a == FA - 1 and dc == 3)
                            out = BANK[bs[dc]][:, 0:n]
                            fn = (lambda e, out=out, l_=wt[:, a, dc * 128:(dc + 1) * 128], r_=ACTb[:, fc, 0:n],
                                  first=first, last=last: e.matmul(out, l_, r_, start=first, stop=last))
                            h = S.op("pe", fn, reads=[rSLOT[s], rACT[fc]] if (first or dc == 0) else [],
                                     writes=[rBANK[bs[dc]]] if first else [], track=False)
                            if last:
                                rBANK[bs[dc]].w = h
                                rBANK[bs[dc]].r = []
                            if lastslot:
                                rSLOT[s].r.append(h)
                            if dg == 3 and dc == 3:
                                rACT[fc].r.append(h)
                for dc in range(4):
                    c = dg * 4 + dc
                    S.op("dve", (lambda e, c=c, b=bs[dc]: e.scalar_tensor_tensor(
                        R[:, c, c0:c0 + n], BANK[b][:, 0:n], 0.5, R[:, c, c0:c0 + n], ALU.mult, ALU.add)),
                        reads=[rBANK[bs[dc]]], writes=[rR[c]])

        def load_x(i):
            blocks = [(0, 128), (128, 128), (256, 128), (384, W - 384)]
            deps0 = []
            for r in rACT[0:36]:
                deps0 += r.wdeps()
            hx = []
            for tb, (t0, nt) in enumerate(blocks):
                st = ACTf32[:, tb * 2048:(tb + 1) * 2048]
                hx.append(S.op("sp", (lambda e, st=st, t0=t0, nt=nt: e.dma_start(out=st[0:nt, :], in_=xin[i, t0:t0 + nt, :])),
                               extra=deps0, key=f"x{tb}"))
            for r in rACT[0:36]:
                r.w = None
                r.r = []
            for tb, (t0, nt) in enumerate(blocks):
                st = ACTf32[:, tb * 2048:(tb + 1) * 2048]
                for cg in range(4):
                    b = nbank()
                    for kk in range(4):
                        c = cg * 4 + kk
                        fn = (lambda e, b=b, kk=kk, st=st, c=c, nt=nt: e.transpose(
                            BANK[b][:, kk * 128:kk * 128 + nt], st[0:nt, c * 128:(c + 1) * 128], IDN[0:nt, 0:nt]))
                        S.op("pe", fn, writes=[rBANK[b]], extra=const_deps + [hx[tb]], wait=(kk == 0), track=(kk == 3))
                    dst = R[:, cg * 4:cg * 4 + 4, t0:t0 + nt]
                    srcv = BANK[b][:, :].rearrange("p (k t) -> p k t", k=4)[:, :, 0:nt]
                    S.op("dve", (lambda e, dst=dst, srcv=srcv: e.tensor_copy(dst, srcv)),
                         reads=[rBANK[b]], writes=rR[cg * 4:cg * 4 + 4])
            hl = S.ops["pe"][-1][2]
            for r in rACT[0:36]:
                r.r.append(hl)

        def pool_mixer(i):
            n = W
            rms_stats(0, n, 0, DC, lambda c: R[:, c, 0:n], 1)
            HN32 = ACTf32[:, 0:DC * W].rearrange("p (c t) -> p c t", c=DC)
            Y = HN
            for c in range(DC):
                S.op("dve", (lambda e, c=c: e.scalar_tensor_tensor(
                    HN32[:, c, :], R[:, c, 0:n], GN[:, G_MIX + c:G_MIX + c + 1], RS[1][:, 0:n], ALU.mult, ALU.mult)),
                    reads=[rR[c], rRS[1]], writes=[rACT[2 * c], rACT[2 * c + 1]], extra=const_deps)
            s = wload(("pool",))
            wt = RING[s][:, :].rearrange("p (g kc d) -> p g kc d", g=4, kc=4, d=512)
            var = 0 if i == 0 else 1
            for gi in range(4):
                h4 = HN32[:, 4 * gi:4 * gi + 4, :]
                hres = [rACT[x] for x in range(8 * gi, 8 * gi + 8)]
                bufs = [(WA, rWA), (WB, rWB)]
                src, sres = h4, hres
                sh = 1
                lo = 1
                for lev in range(gi + 1):
                    dstb, dres = bufs[lev % 2]
                    S.op("dve", (lambda e, dstb=dstb, src=src, lo=lo, sh=sh: e.tensor_tensor(
                        dstb[:, :, lo:W], src[:, :, lo:W], src[:, :, lo - sh:W - sh], ALU.add)),
                        reads=sres, writes=[dres])
                    src, sres = dstb, [dres]
                    sh *= 2
                    lo = 2 * sh - 1
                for kk in range(4):
                    c = 4 * gi + kk
                    S.op("dve", (lambda e, src=src, kk=kk, gi=gi: e.tensor_tensor(
                        src[:, kk, HALO:W], src[:, kk, HALO:W], INVC[:, var, gi, :], ALU.mult)),
                        reads=[], writes=sres, extra=const_deps)
                    S.op("dve", (lambda e, src=src, kk=kk, c=c: e.tensor_tensor(
                        Y[:, c, 0:T], src[:, kk, HALO:W], HN32[:, c, HALO:W], ALU.subtract)),
                        reads=sres + [rACT[2 * c], rACT[2 * c + 1]], writes=[rHN[c]])
                for dc in range(4):
                    b = nbank()
                    mm_group(b, 128, T, [(wt[:, gi, kc, dc * 128:(dc + 1) * 128], Y[:, 4 * gi + kc, 0:T]) for kc in range(4)],
                             reads=[rSLOT[s]] + rHN[4 * gi:4 * gi + 4])
                    c = 4 * gi + dc
                    S.op("dve", (lambda e, c=c, b=b: e.scalar_tensor_tensor(
                        R[:, c, HALO:W], BANK[b][:, 0:T], GN[:, G_PSC + c:G_PSC + c + 1], R[:, c, HALO:W],
                        ALU.mult, ALU.add)), reads=[rBANK[b]], writes=[rR[c]])

        kv_handles = {}
        cc_handles = []

        def exchange(i):
            if not fused:
                return
            RG = [[2 * q_, 2 * q_ + 1] for q_ in range(nseq)]
            cc_handles.append(S.op("pool", (lambda e: e.collective_compute(
                "AllGather", ALU.bypass, replica_groups=RG,
                ins=[kmine_ts[i].ap().opt()], outs=[kbuf_ts[i].ap().opt()])), extra=kv_handles[i], key="cc", unit=1))
            cc_handles.append(S.op("pool", (lambda e: e.collective_compute(
                "AllGather", ALU.bypass, replica_groups=RG,
                ins=[vmine_ts[i].ap().opt()], outs=[vbuf_ts[i].ap().opt()])), extra=kv_handles[i], key="cc", unit=1))

        def qkv(i):
            n = T
            rms_stats(HALO, n, 0, DC, lambda c: R[:, c, HALO:W], 0)
            norm_to_hn(HALO, n, G_MIX + 16, 0)
            for which, gcol, base, sq_scale in (("q", G_QN, 0, 128.0), ("k", G_KN, 16, 1.0)):
                for hq in range(4):
                    s = wload((which, hq))
                    wt = RING[s][:, :].rearrange("p (c f) -> p c f", c=16)
                    for hh in range(4):
                        h = hq * 4 + hh
                        b = nbank()
                        mm_group(b, 128, n, [(wt[:, c, hh * 128:(hh + 1) * 128], HN[:, c, 0:n]) for c in range(DC)],
                                 reads=[rSLOT[s]] + rHN)
                        qq = h % 2
                        tmp = 42 + (h % 2)
                        S.op("dve", (lambda e, b=b, qq=qq: e.tensor_copy(SG[qq][:, 0:n], BANK[b][:, 0:n])),
                             reads=[rBANK[b]], writes=[rSG[qq]])
                        S.op("act", (lambda e, qq=qq, tmp=tmp: e.activation(ACTb[:, tmp, 0:n], SG[qq][:, 0:n], AF.Square)),
                             reads=[rSG[qq]], writes=[rACT[tmp]])
                        b2 = nbank()
                        mm_group(b2, 128, n, [(CBF[:, 1, :], ACTb[:, tmp, 0:n])], reads=[rACT[tmp]])
                        S.op("act", (lambda e, b2=b2, qq=qq, sq_scale=sq_scale: e.activation(
                            RS[qq][:, 0:n], BANK[b2][:, 0:n], AF.Sqrt, bias=EPS * sq_scale, scale=sq_scale)),
                            reads=[rBANK[b2]], writes=[rRS[qq]])
                        S.op("dve", (lambda e, qq=qq: e.reciprocal(RS[qq][:, 0:n], RS[qq][:, 0:n])),
                             reads=[rRS[qq]], writes=[rRS[qq]])
                        S.op("dve", (lambda e, qq=qq, h=h, base=base, gcol=gcol: e.scalar_tensor_tensor(
                            ACTb[:, base + h, 0:n], SG[qq][:, 0:n], GN[:, gcol:gcol + 1], RS[qq][:, 0:n],
                            ALU.mult, ALU.mult)), reads=[rSG[qq], rRS[qq]], writes=[rACT[base + h]], extra=const_deps)
            if STOP < 7:
                return
            S.op("sp", (lambda e: e.dma_start(out=qs_w[i].rearrange("p (h t) -> p h t", h=16), in_=ACTb[:, 0:16, 0:T])),
                 reads=rACT[0:16], key="qst")
            S.op("sp", (lambda e: e.dma_start(out=kmine_ap(i).rearrange("p (h t) -> p h t", h=16), in_=ACTb[:, 16:32, 0:T])),
                 reads=rACT[16:32], key="kst")
            kv_handles[i] = [S.ops["sp"][-1][2]]
            if STOP < 8:
                return
            for hq in range(4):
                s = wload(("v", hq))
                wt = RING[s][:, :].rearrange("p (c f) -> p c f", c=16)
                vq = hq % 2
                c_lo = 32 + 5 * vq
                vres = rACT[c_lo:c_lo + 5]
                VQ = ACTflat[:, c_lo * W:c_lo * W + 2048].rearrange("p (h kb f) -> p h kb f", h=4, kb=4, f=128)
                hs = []
                for kb in range(NKB):
                    b = nbank()
                    mm_group(b, KB, 512, [(HN[:, c, kb * KB:(kb + 1) * KB], wt[:, c, :]) for c in range(DC)],
                             reads=[rSLOT[s]] + rHN)
                    dst = VQ[0:KB, :, kb, :]
                    srcv = BANK[b][0:KB, :].rearrange("p (h f) -> p h f", h=4)
                    if kb % 2 == 0:
                        hs.append(S.op("dve", (lambda e, dst=dst, srcv=srcv: e.tensor_copy(dst, srcv)),
                                       reads=[rBANK[b]], writes=vres, track=False))
                    else:
                        hs.append(S.op("act", (lambda e, dst=dst, srcv=srcv: e.activation(dst, srcv, AF.Copy)),
                                       reads=[rBANK[b]], writes=vres, track=False))
                    rBANK[b].r.append(hs[-1])
                if STOP < 9:
                    continue
                hv = S.op("sp", (lambda e, hq=hq, c_lo=c_lo: e.dma_start(
                    out=vmine_ap(i)[:, hq * 2048:(hq + 1) * 2048], in_=ACTflat[0:KB, c_lo * W:c_lo * W + 2048])),
                    extra=hs, key=f"vst{vq}")
                for r in vres:
                    r.w = None
                    r.r = [hv]
                kv_handles[i].append(hv)


        def phaseA_tile(i):
            if STOP >= 2:
                load_x(i)
            if STOP >= 3:
                ffn(0, 0, 0, W)
            if i > 0:
                exchange(i - 1)
            if STOP >= 4:
                pool_mixer(i)
            if STOP >= 5:
                ffn(0, 1, HALO, T)
                ffn(1, 0, HALO, T)
            S.op("sp", (lambda e: e.dma_start(out=h1s_w[i].rearrange("p (c t) -> p c t", c=DC), in_=R[:, :, HALO:W])),
                 reads=rR, key="h1st")
            if STOP >= 6:
                qkv(i)

        rKT = [Res() for _ in range(4)]
        rV = [Res() for _ in range(4)]
        rE = [Res() for _ in range(3)]
        rSP = [Res() for _ in range(3)]
        rS = [Res() for _ in range(3)]
        rEX = [Res() for _ in range(3)]
        rA = [Res() for _ in range(4)]
        ZB, GB, OB = (0, 1, 2), (3, 4, 7), (5, 6)
        cnt = {"kv": 0, "blk": 0, "S": 0, "o": 0}

        def attention(i, extra_in):
            QT = HN
            OT = ACTb
            nch = 2 * i + 2
            blocks = []
            for h in range(16):
                for jj in range(nch - 1, -1, -1):
                    for kb in range(NKB - 1, -1, -1):
                        blocks.append((h, jj, kb))
            nb = len(blocks)
            st = {}

            nvis = nb // NKB

            def issue_kv(vv):
                h, jj, _ = blocks[vv * NKB]
                q = cnt["kv"] % 4
                cnt["kv"] += 1
                st[("kv", h, jj)] = q
                S.op("sp", (lambda e, q=q, h=h, jj=jj: e.dma_start(
                    out=KTb[q][:, :], in_=kbuf_ap(jj).rearrange("p (h t) -> p h t", h=16)[:, h, :])),
                    writes=[rKT[q]], extra=extra_in, key=f"kl{q}")
                S.op("sp", (lambda e, q=q, h=h, jj=jj: e.dma_start(
                    out=Vb[q][:, :, :], in_=vbuf_ap(jj).rearrange("p (h kb f) -> p h kb f", h=16, kb=4)[:, h, :, :])),
                    writes=[rV[q]], extra=extra_in, key=f"vl{q}")

            def stage_z(bi):
                h, jj, kb = blocks[bi]
                if kb == NKB - 1:
                    v0 = bi // NKB
                    for vv in ([0, 1, 2] if v0 == 0 else [v0 + 2]):
                        if vv < nvis:
                            issue_kv(vv)
                q = st[("kv", h, jj)]
                zb = ZB[bi % 3]
                S.op("pe", (lambda e, zb=zb, q=q, kb=kb, h=h: e.matmul(
                    BANK[zb][0:KB, 0:T], KTb[q][:, kb * KB:(kb + 1) * KB], QT[:, h, 0:T], start=True, stop=True)),
                    reads=[rKT[q], rHN[h]], writes=[rBANK[zb]])
                ei = bi % 3
                S.op("act", (lambda e, ei=ei, zb=zb: e.activation(Eb[ei][:, :], BANK[zb][0:KB, 0:T], AF.Exp)),
                     reads=[rBANK[zb]], writes=[rE[ei]])
                if jj >= nch - 2:
                    slot = 0 if jj == nch - 1 else 1
                    S.op("pool", (lambda e, ei=ei, slot=slot, kb=kb: e.tensor_tensor(
                        Eb[ei][:, :], Eb[ei][:, :], MASK[:, slot * NKB + kb, :], ALU.mult)),
                        reads=[], writes=[rE[ei]], extra=const_deps)
                S.op("act", (lambda e, ei=ei: e.activation(SPb[ei][:, :], Eb[ei][:, :], AF.Ln, bias=1.0, scale=1.0)),
                     reads=[rE[ei]], writes=[rSP[ei]])

            def stage_g(bi):
                h, jj, kb = blocks[bi]
                first = (jj == nch - 1 and kb == NKB - 1)
                ei = bi % 3
                gb = GB[bi % 3]
                if first:
                    S.op("pe", (lambda e, gb=gb, ei=ei: e.matmul(
                        BANK[gb][0:KB, 0:T], CBF[0:KB, 2, 0:KB], SPb[ei][:, :], start=True, stop=True)),
                        reads=[rSP[ei]], writes=[rBANK[gb]], extra=const_deps)
                    si = cnt["S"] % 3
                    cnt["S"] += 1
                    S.op("pool", (lambda e, si=si, ei=ei: e.tensor_copy(Sb[si][:, :], SPb[ei][:, :])),
                         reads=[rSP[ei]], writes=[rS[si]])
                else:
                    sprev = (cnt["S"] - 1) % 3
                    S.op("pe", (lambda e, gb=gb, ei=ei: e.matmul(
                        BANK[gb][0:KB, 0:T], CBF[0:KB, 2, 0:KB], SPb[ei][:, :], start=True, stop=False)),
                        reads=[rSP[ei]], writes=[rBANK[gb]], extra=const_deps)
                    S.op("pe", (lambda e, gb=gb, sprev=sprev: e.matmul(
                        BANK[gb][0:KB, 0:T], CBF[0:KB, 3, 0:KB], Sb[sprev][:, :], start=False, stop=True)),
                        reads=[rS[sprev]], writes=[rBANK[gb]])
                    last = (jj == 0 and kb == 0)
                    if not last:
                        si = cnt["S"] % 3
                        cnt["S"] += 1
                        S.op("pool", (lambda e, si=si, sprev=sprev, ei=ei: e.tensor_tensor(
                            Sb[si][:, :], Sb[sprev][:, :], SPb[ei][:, :], ALU.add)),
                            reads=[rSP[ei], rS[sprev]], writes=[rS[si]])

            def stage_a(bi):
                ei = bi % 3
                gb = GB[bi % 3]
                xi = bi % 3
                ai = bi % 4
                S.op("act", (lambda e, xi=xi, gb=gb: e.activation(EXb[xi][:, :], BANK[gb][0:KB, 0:T], AF.Exp)),
                     reads=[rBANK[gb]], writes=[rEX[xi]])
                S.op("dve", (lambda e, ei=ei, xi=xi, ai=ai: e.tensor_tensor(Ab[ai][:, :], Eb[ei][:, :], EXb[xi][:, :], ALU.mult)),
                     reads=[rE[ei], rEX[xi]], writes=[rA[ai]])

            def stage_o(bi):
                h, jj, kb = blocks[bi]
                first = (jj == nch - 1 and kb == NKB - 1)
                last = (jj == 0 and kb == 0)
                ai = bi % 4
                q = st[("kv", h, jj)]
                ob = OB[h % 2]
                S.op("pe", (lambda e, ob=ob, q=q, kb=kb, ai=ai, first=first, last=last: e.matmul(
                    BANK[ob][:, 0:T], Vb[q][:, kb, :], Ab[ai][:, :], start=first, stop=last)),
                    reads=[rV[q], rA[ai]], writes=[rBANK[ob]] if (first or last) else [])
                if last:
                    S.op("dve", (lambda e, ob=ob, h=h: e.tensor_copy(OT[:, h, 0:T], BANK[ob][:, 0:T])),
                         reads=[rBANK[ob]], writes=[rACT[h]])

            for kk in range(nb + 4):
                if kk < nb:
                    stage_z(kk)
                if 0 <= kk - 1 < nb:
                    stage_g(kk - 1)
                if 0 <= kk - 2 < nb:
                    stage_a(kk - 2)
                if 0 <= kk - 4 < nb:
                    stage_o(kk - 4)

        def oproj():
            for dg in range(4):
                s = wload(("o", dg))
                wt = RING[s][:, :].rearrange("p (c f) -> p c f", c=16)
                for dc in range(4):
                    b = nbank()
                    mm_group(b, 128, T, [(wt[:, hc, dc * 128:(dc + 1) * 128], ACTb[:, hc, 0:T]) for hc in range(16)],
                             reads=[rSLOT[s]] + rACT[0:16])
                    c = dg * 4 + dc
                    S.op("dve", (lambda e, c=c, b=b: e.tensor_tensor(
                        R[:, c, HALO:W], R[:, c, HALO:W], BANK[b][:, 0:T], ALU.add)),
                        reads=[rBANK[b]], writes=[rR[c]])

        def store_out(i):
            OS = ACTf32[:, 0:8192].rearrange("p (kb d) -> p kb d", kb=4)
            ocopies = []
            for kb in range(NKB):
                for cg in range(4):
                    b = nbank()
                    for kk in range(4):
                        c = cg * 4 + kk
                        fn = (lambda e, b=b, kk=kk, c=c, kb=kb: e.transpose(
                            BANK[b][0:KB, kk * 128:(kk + 1) * 128], R[:, c, HALO + kb * KB:HALO + (kb + 1) * KB], IDN[:, :]))
                        S.op("pe", fn, reads=[rR[c]], writes=[rBANK[b]], extra=const_deps)
                    osv = OS[0:KB, kb, cg * 512:(cg + 1) * 512]
                    if cg % 2 == 0:
                        hc = S.op("dve", (lambda e, b=b, osv=osv: e.tensor_copy(osv, BANK[b][0:KB, :])),
                                  reads=[rBANK[b]], writes=rACT[0:36], track=False)
                    else:
                        hc = S.op("act", (lambda e, b=b, osv=osv: e.activation(osv, BANK[b][0:KB, :], AF.Copy)),
                                  reads=[rBANK[b]], writes=rACT[0:36], track=False)
                    rBANK[b].r.append(hc)
                    ocopies.append(hc)
            S.op("sp", (lambda e: e.dma_start(out=yout[i].rearrange("(kb p) d -> p kb d", p=KB), in_=OS[0:KB, :, :])),
                 extra=ocopies, key="yst")
            hy = S.ops["sp"][-1][2]
            for x in range(0, 36):
                rACT[x].w = None
                rACT[x].r = [hy]
            return hy

        def phaseB_tile(i, extra_in):
            S.op("sp", (lambda e: e.dma_start(out=R[:, :, HALO:W], in_=h1s_r[i].rearrange("p (c t) -> p c t", c=DC))),
                 writes=rR, extra=extra_in, key="h1ld")
            S.op("sp", (lambda e: e.dma_start(out=HN[:, :, 0:T], in_=qs_r[i].rearrange("p (h t) -> p h t", h=16))),
                 writes=rHN, extra=extra_in, key="qld")
            attention(i, extra_in)
            oproj()
            ffn(1, 1, HALO, T)
            return store_out(i)

        a_done = []
        if doA:
            for i in range(NT):
                phaseA_tile(i)
            a_done = [h for (_, _, h) in S.ops["sp"] if h.key in ("h1st", "qst", "kst", "vst0", "vst1")]
        last_out = None
        if doB:
            extra_in = []
            if fused:
                exchange(NT - 1)
                extra_in = [cc_handles[-1]] + a_done
            for i in range(NT):
                last_out = phaseB_tile(i, extra_in)
        finals = a_done if not doB else [h for (_, _, h) in S.ops["sp"] if h.key == "yst"]
        S.ops["sp"].append(((lambda e: _Nop()), finals, Hd("sp", len(S.ops["sp"]))))

        keys = sorted(S.dma_count.keys())
        sems = {e: es.enter_context(nc.semaphore(f"s_{e}")) for e in Sched.ENGS}
        dsems = {k: es.enter_context(nc.semaphore(f"d_{k}")) for k in keys}
        block = es.enter_context(nc.Block())
        S.emit(nc, block, sems, dsems)
    return nc


class _Nop:
    def then_inc(self, *a, **k):
        return self


def _consts():
    ident = np.eye(128, dtype=np.float32)
    cbf = np.zeros((128, 4, 128), dtype=np.float32)
    cbf[:, 0, :] = 1.0 / D
    cbf[:, 1, :] = 1.0 / 128
    j = np.arange(128)[:, None]
    m = np.arange(128)[None, :]
    cbf[:, 2, :] = -(j >= m).astype(np.float32)
    cbf[:, 3, :] = -1.0
    return ident, cbf.astype(ml_dtypes.bfloat16)


def _fm(v):
    return np.ascontiguousarray(np.asarray(v, dtype=np.float32).reshape(DC, 128).T)


def kernel(x, meta, ffn_norm, ffn_w_in, ffn_w_out, mix_norm, pool_w, pool_scale, sb_w_qkv, sb_qk_norm, sb_w_o):
    x = np.asarray(x, dtype=np.float32)
    B = x.shape[0]
    ident, cbf = _consts()
    gains = np.zeros((128, NG), dtype=np.float32)
    for l in range(2):
        for j in range(2):
            gains[:, G_FFN + (l * 2 + j) * 16: G_FFN + (l * 2 + j) * 16 + 16] = _fm(ffn_norm[l, j])
        gains[:, G_MIX + l * 16: G_MIX + l * 16 + 16] = _fm(mix_norm[l])
    gains[:, G_PSC:G_PSC + 16] = _fm(pool_scale[0])
    gains[:, G_QN] = np.asarray(sb_qk_norm[0, 0], dtype=np.float32)
    gains[:, G_KN] = np.asarray(sb_qk_norm[0, 1], dtype=np.float32)

    invc = np.zeros((2, 128, 4, T), dtype=np.float32)
    pos = np.arange(T)
    for gi, w in enumerate((2, 4, 8, 16)):
        invc[0, :, gi, :] = 1.0 / np.minimum(pos + 1, w)
        invc[1, :, gi, :] = 1.0 / w
    s_idx = np.arange(KB)[:, None]
    t_idx = np.arange(T)[None, :]
    diag = np.stack([((kb * KB + s_idx) < t_idx) for kb in range(NKB)], axis=1).astype(np.float32)
    masks = []
    for p in range(2):
        m = np.zeros((KB, 2 * NKB, T), dtype=np.float32)
        if p == 0:
            m[:, NKB:, :] = diag
        else:
            m[:, :NKB, :] = diag
            m[:, NKB:, :] = 1.0
        masks.append(m.astype(ml_dtypes.bfloat16))

    ncores = 2 * B
    seq = x.shape[1]
    assert NMETA + seq == 2 * NT * T
    xins = []
    for c in range(ncores):
        s, p = c // 2, c % 2
        full = np.concatenate([np.zeros((HALO, D), np.float32), np.asarray(meta, np.float32), x[s]], axis=0)
        xi = np.empty((NT, W, D), dtype=np.float32)
        for i in range(NT):
            g = 2 * i + p
            xi[i] = full[T * g: T * g + W]
        xins.append(xi)

    w_in_np = np.asarray(ffn_w_in, dtype=np.float32)
    w_out_np = np.asarray(ffn_w_out, dtype=np.float32)
    common = {"ffn_w_in": w_in_np, "ffn_w_out": w_out_np, "gains": gains, "ident": ident, "cbf": cbf}

    if FUSED:
        ncF = build_program("AB", nseq=B)
        mapsF = []
        for c in range(ncores):
            p = c % 2
            m = dict(common)
            invc_c = invc if (p == 0) else np.stack([invc[1], invc[1]], axis=0)
            m.update({"xin": xins[c], "invc": invc_c, "pool_w": np.asarray(pool_w[0], np.float32),
                      "sb_w_qkv": np.asarray(sb_w_qkv[0], np.float32),
                      "sb_w_o": np.asarray(sb_w_o[0], np.float32), "mask": masks[p]})
            mapsF.append(m)
        resB = run_bass_kernel_spmd(ncF, mapsF, core_ids=list(range(ncores))).results
    else:
        ncA = build_program("A")
        mapsA = []
        for c in range(ncores):
            m = dict(common)
            invc_c = invc if (c % 2 == 0) else np.stack([invc[1], invc[1]], axis=0)
            m.update({"xin": xins[c], "invc": invc_c, "pool_w": np.asarray(pool_w[0], np.float32),
                      "sb_w_qkv": np.asarray(sb_w_qkv[0], np.float32)})
            mapsA.append(m)
        resA = run_bass_kernel_spmd(ncA, mapsA, core_ids=list(range(ncores))).results

        ncB = build_program("B")
        mapsB = []
        for c in range(ncores):
            s, p = c // 2, c % 2
            m = dict(common)
            kb_ = np.concatenate([resA[2 * s]["kmine"], resA[2 * s + 1]["kmine"]], axis=0)
            vb_ = np.concatenate([resA[2 * s]["vmine"], resA[2 * s + 1]["vmine"]], axis=0)
            m.update({"sb_w_o": np.asarray(sb_w_o[0], np.float32), "mask": masks[p],
                      "h1s": resA[c]["h1s"], "qs": resA[c]["qs"], "kbuf": kb_, "vbuf": vb_})
            mapsB.append(m)
        resB = run_bass_kernel_spmd(ncB, mapsB, core_ids=list(range(ncores))).results

    out = np.empty((B, seq, D), dtype=np.float32)
    for c in range(ncores):
        s, p = c // 2, c % 2
        y = resB[c]["y"]
        for i in range(NT):
            g = 2 * i + p
            p0 = T * g - NMETA
            if p0 < 0:
                out[s, 0:T - NMETA] = y[i, NMETA:]
            else:
                out[s, p0:p0 + T] = y[i]
    return out
```

```python
import contextlib
import numpy as np
import ml_dtypes
import concourse.bass as bass
import concourse.mybir as mybir
from concourse.bass_utils import run_bass_kernel_spmd

F32 = mybir.dt.float32
BF16 = mybir.dt.bfloat16
AF = mybir.ActivationFunctionType
ALU = mybir.AluOpType

D = 2048
DC = 16
FF = 5632
FC = 44
FA = 11
ARC = 44
T = 456
HALO = 16
W = T + HALO
NT = 9
KB = 114
NKB = 4
NCH = 18
SEQ = 8192
NMETA = 16
EPS = 1e-6
NSLOT = 3
SAME_ENGINE_SYNC = False
FUSED = True

G_FFN = 0
G_MIX = 64
G_PSC = 96
G_QN = 112
G_KN = 113
NG = 114


class Hd:
    __slots__ = ("eng", "idx", "key", "ord", "sig", "unit")

    def __init__(self, eng, idx, key=None, ordn=0, unit=16):
        self.eng, self.idx, self.key, self.ord, self.sig, self.unit = eng, idx, key, ordn, 0, unit


class Res:
    def __init__(self):
        self.w = None
        self.r = []

    def rdeps(self):
        return [self.w] if self.w is not None else []

    def wdeps(self):
        return ([self.w] if self.w is not None else []) + self.r


class Sched:
    ENGS = ("sp", "pe", "act", "dve", "pool")

    def __init__(self):
        self.ops = {e: [] for e in self.ENGS}
        self.dma_count = {}

    def op(self, eng, fn, reads=(), writes=(), extra=(), key=None, wait=True, track=True, unit=16):
        deps = list(extra)
        if wait:
            for r in reads:
                deps += r.rdeps()
            for w in writes:
                deps += w.wdeps()
        if key is not None:
            self.dma_count[key] = self.dma_count.get(key, 0) + 1
            h = Hd(eng, len(self.ops[eng]), key, self.dma_count[key], unit)
        else:
            h = Hd(eng, len(self.ops[eng]))
        self.ops[eng].append((fn, deps, h))
        if track:
            for r in reads:
                r.r.append(h)
            for w in writes:
                w.w = h
                w.r = []
        return h

    def emit(self, nc, block, sems, dsems):
        need = set()
        for e in self.ENGS:
            for fn, deps, h in self.ops[e]:
                for d in deps:
                    if d.key is None:
                        if d.eng == e and (e == "pe" or not SAME_ENGINE_SYNC):
                            continue
                        need.add(id(d))
        for e in self.ENGS:
            cnt = 0
            for fn, deps, h in self.ops[e]:
                if h.key is None and id(h) in need:
                    cnt += 1
                    h.sig = cnt

        def run(e, engobj):
            waited = {}
            for fn, deps, h in self.ops[e]:
                for d in deps:
                    if d.key is not None:
                        sk, val, sem = ("d", d.key), d.unit * d.ord, dsems[d.key]
                    else:
                        if d.eng == e and (e == "pe" or not SAME_ENGINE_SYNC):
                            continue
                        sk, val, sem = ("e", d.eng), d.sig, sems[d.eng]
                    if waited.get(sk, 0) >= val:
                        continue
                    engobj.wait_ge(sem, val)
                    waited[sk] = val
                ins = fn(engobj)
                if h.key is not None:
                    ins.then_inc(dsems[h.key], h.unit)
                elif h.sig:
                    ins.then_inc(sems[e], 1)

        @block.sync
        def _(eng):
            run("sp", eng)

        @block.tensor
        def _(eng):
            run("pe", eng)

        @block.scalar
        def _(eng):
            run("act", eng)

        @block.vector
        def _(eng):
            run("dve", eng)

        @block.gpsimd
        def _(eng):
            run("pool", eng)


def build_program(mode, nseq=4):
    nc = bass.Bass("TRN2", target_bir_lowering=False)
    doA = "A" in mode
    doB = "B" in mode
    fused = mode == "AB"

    def dram(name, shape, dt, kind):
        return nc.dram_tensor(name, list(shape), dt, kind=kind).ap()

    IN, OUT, INT = "ExternalInput", "ExternalOutput", "Internal"
    w_in = dram("ffn_w_in", [2, 2, D, 2 * FF], F32, IN)
    w_out = dram("ffn_w_out", [2, 2, FF, D], F32, IN)
    gains_d = dram("gains", [128, NG], F32, IN)
    ident_d = dram("ident", [128, 128], F32, IN)
    cbf_d = dram("cbf", [128, 4, 128], BF16, IN)
    if doA:
        xin = dram("xin", [NT, W, D], F32, IN)
        invc_d = dram("invc", [2, 128, 4, T], F32, IN)
        pool_w = dram("pool_w", [4, 512, 512], F32, IN)
        w_qkv = dram("sb_w_qkv", [D, 3 * D], F32, IN)
    if doB:
        w_o = dram("sb_w_o", [D, D], F32, IN)
        mask_d = dram("mask", [KB, 2 * NKB, T], BF16, IN)
        yout = dram("y", [NT, T, D], F32, OUT)
    KW = DC * T
    VW = DC * NKB * 128
    if fused:
        h1s_w = nc.dram_tensor("h1s", [NT, 128, KW], F32).ap()
        qs_w = nc.dram_tensor("qs", [NT, 128, KW], BF16).ap()
        kmine_ts = [nc.dram_tensor(f"kmine{i}", [128, KW], BF16) for i in range(NT)]
        vmine_ts = [nc.dram_tensor(f"vmine{i}", [KB, VW], BF16) for i in range(NT)]
        kbuf_ts = [nc.dram_tensor(f"kbuf{i}", [2 * 128, KW], BF16) for i in range(NT)]
        vbuf_ts = [nc.dram_tensor(f"vbuf{i}", [2 * KB, VW], BF16) for i in range(NT)]
        h1s_r, qs_r = h1s_w, qs_w
    else:
        if doA:
            h1s_w = dram("h1s", [NT, 128, KW], F32, OUT)
            qs_w = dram("qs", [NT, 128, KW], BF16, OUT)
            kmine = dram("kmine", [NT * 128, KW], BF16, OUT)
            vmine = dram("vmine", [NT * KB, VW], BF16, OUT)
        if doB:
            h1s_r = dram("h1s", [NT, 128, KW], F32, IN)
            qs_r = dram("qs", [NT, 128, KW], BF16, IN)
            kbuf = dram("kbuf", [2 * NT * 128, KW], BF16, IN)
            vbuf = dram("vbuf", [2 * NT * KB, VW], BF16, IN)

    def kmine_ap(i):
        return kmine_ts[i].ap() if fused else kmine[i * 128:(i + 1) * 128, :]

    def vmine_ap(i):
        return vmine_ts[i].ap() if fused else vmine[i * KB:(i + 1) * KB, :]

    def kbuf_ap(jj):
        if fused:
            return kbuf_ts[jj // 2].ap()[(jj % 2) * 128:(jj % 2 + 1) * 128, :]
        r = (jj % 2) * NT + jj // 2
        return kbuf[r * 128:(r + 1) * 128, :]

    def vbuf_ap(jj):
        if fused:
            return vbuf_ts[jj // 2].ap()[(jj % 2) * KB:(jj % 2 + 1) * KB, :]
        r = (jj % 2) * NT + jj // 2
        return vbuf[r * KB:(r + 1) * KB, :]

    tiles = []

    def c_p(ap2d):
        return ap2d.rearrange("(c p) n -> p c n", p=128)

    def add_win(l, j):
        src = c_p(w_in[l, j])
        for ftp in range(FC // 2):
            halves = []
            for hh in range(2):
                parts = []
                for gu in range(2):
                    s = src[:, 8 * hh:8 * hh + 8, gu * FF + 256 * ftp: gu * FF + 256 * ftp + 256]
                    parts.append((("win", gu), s))
                halves.append((hh * 4096, 4096, parts))
            tiles.append((("win", l, j, ftp), halves))

    def add_wout(l, j):
        src = c_p(w_out[l, j])
        for dg in range(4):
            for fq in range(4):
                halves = []
                h0 = (FA + 1) // 2
                for (a, n) in (((0, h0), (h0, FA - h0)) if FA > 1 else ((0, 1),)):
                    s = src[:, fq * FA + a: fq * FA + a + n, dg * 512:(dg + 1) * 512]
                    halves.append((a * 512, n * 512, [(("flat", n, 512), s)]))
                tiles.append((("wout", l, j, dg, fq), halves))

    def add_cp(name, ap2d, col0):
        src = c_p(ap2d)
        halves = []
        for hh in range(2):
            s = src[:, 8 * hh:8 * hh + 8, col0:col0 + 512]
            halves.append((hh * 4096, 4096, [(("flat", 8, 512), s)]))
        tiles.append((name, halves))

    if doA:
        add_win(0, 0); add_wout(0, 0)
        src = pool_w.rearrange("g (kc p) d -> p g kc d", p=128)
        halves = []
        for hh in range(2):
            halves.append((hh * 4096, 4096, [(("pool",), src[:, 2 * hh:2 * hh + 2, :, :])]))
        tiles.append((("pool",), halves))
        add_win(0, 1); add_wout(0, 1)
        add_win(1, 0); add_wout(1, 0)
        for hq in range(4):
            add_cp(("q", hq), w_qkv, hq * 512)
        for hq in range(4):
            add_cp(("k", hq), w_qkv, D + hq * 512)
        for hq in range(4):
            add_cp(("v", hq), w_qkv, 2 * D + hq * 512)
    if doB:
        for dg in range(4):
            add_cp(("o", dg), w_o, dg * 512)
        add_win(1, 1); add_wout(1, 1)
    tile_idx = {name: i for i, (name, _) in enumerate(tiles)}
    NWT = len(tiles)
    WPART = 64
    wsc_parts = [dram(f"wsc{q_}", [min(WPART, NWT - q_ * WPART), 128, 8192], BF16, INT)
                 for q_ in range((NWT + WPART - 1) // WPART)]

    def wsc_t(ti):
        return wsc_parts[ti // WPART][ti % WPART]

    S = Sched()
    es = contextlib.ExitStack()

    def sb(name, shape, dt):
        return es.enter_context(nc.sbuf_tensor(name, list(shape), dt))

    with es:
        R = sb("R", [128, DC, W], F32)
        HN = sb("HN", [128, DC, W], BF16)
        ACTb = sb("ACTb", [128, ARC, W], BF16)
        RING = [sb(f"ring{i}", [128, 8192], BF16) for i in range(NSLOT)]
        GN = sb("GN", [128, NG], F32)
        IDN = sb("IDN", [128, 128], F32)
        CBF = sb("CBF", [128, 4, 128], BF16)
        RS = [sb(f"RS{i}", [128, W], F32) for i in range(2)]
        SG = [sb(f"SG{i}", [128, W], F32) for i in range(2)]
        if doA:
            WA = sb("WA", [128, 4, W], F32)
            WB = sb("WB", [128, 4, W], F32)
            INVC = sb("INVC", [128, 2, 4, T], F32)
        if doB:
            MASK = sb("MASK", [KB, 2 * NKB, T], BF16)
            KTb = [sb(f"KT{i}", [128, T], BF16) for i in range(4)]
            Vb = [sb(f"V{i}", [KB, NKB, 128], BF16) for i in range(4)]
            Eb = [sb(f"E{i}", [KB, T], F32) for i in range(3)]
            SPb = [sb(f"SP{i}", [KB, T], BF16) for i in range(3)]
            Sb = [sb(f"S{i}", [KB, T], BF16) for i in range(3)]
            EXb = [sb(f"EX{i}", [KB, T], BF16) for i in range(3)]
            Ab = [sb(f"A{i}", [KB, T], BF16) for i in range(4)]
        BANK = [es.enter_context(nc.psum_tensor(f"bank{i}", [128, 512], F32)) for i in range(8)]


        rR = [Res() for _ in range(DC)]
        rHN = [Res() for _ in range(DC)]
        rACT = [Res() for _ in range(ARC)]
        rBANK = [Res() for _ in range(8)]
        rSLOT = [Res() for _ in range(NSLOT)]
        rRS = [Res() for _ in range(2)]
        rSG = [Res() for _ in range(2)]
        rCONST = Res()
        rWA, rWB = Res(), Res()
        rTILE = [Res() for _ in range(NWT)]

        def flat(t):
            return t[:, :, :].rearrange("p a b -> p (a b)")

        ACTflat = flat(ACTb)
        ACTf32 = ACTflat.bitcast(F32)
        HNflat = flat(HN)

        S.op("sp", lambda e: e.dma_start(out=GN[:, :], in_=gains_d[:, :]), writes=[rCONST], key="c0")
        S.op("sp", lambda e: e.dma_start(out=IDN[:, :], in_=ident_d[:, :]), writes=[rCONST], key="c0")
        S.op("sp", lambda e: e.dma_start(out=CBF[:, :, :], in_=cbf_d[:, :, :]), writes=[rCONST], key="c0")
        if doA:
            S.op("sp", lambda e: e.dma_start(out=INVC[:, :, :, :], in_=invc_d.rearrange("v p g t -> p v g t")),
                 writes=[rCONST], key="c0")
        if doB:
            S.op("sp", lambda e: e.dma_start(out=MASK[:, :, :], in_=mask_d[:, :, :]), writes=[rCONST], key="c0")
        const_deps = [S.ops["sp"][-1][2]]

        S32 = [ACTf32[:, 0:4096], ACTf32[:, 4096:8192]]
        S16 = [ACTflat[:, 16384:20480], HNflat[:, 0:4096]]
        rS32 = [Res(), Res()]
        rS16 = [Res(), Res()]
        k = 0
        for ti, (name, halves) in enumerate(tiles):
            for (off, nel, parts) in halves:
                hb = k % 2
                for (kind, src) in parts:
                    if kind[0] == "win":
                        gu = kind[1]
                        dst = S32[hb].rearrange("p (c g f) -> p c g f", c=8, g=2, f=256)[:, :, gu, :]
                    elif kind[0] == "flat":
                        dst = S32[hb][:, 0:kind[1] * kind[2]].rearrange("p (a b) -> p a b", a=kind[1])
                    else:
                        dst = S32[hb].rearrange("p (g kc d) -> p g kc d", g=2, kc=4, d=512)
                    S.op("sp", (lambda e, dst=dst, src=src: e.dma_start(out=dst, in_=src)),
                         writes=[rS32[hb]], key=f"cl{hb}")
                src32 = S32[hb][:, 0:nel]
                dst16 = S16[hb][:, 0:nel]
                if k % 2 == 0:
                    S.op("dve", (lambda e, a=dst16, b=src32: e.tensor_copy(a, b)),
                         reads=[rS32[hb]], writes=[rS16[hb]])
                else:
                    S.op("act", (lambda e, a=dst16, b=src32: e.activation(a, b, AF.Copy)),
                         reads=[rS32[hb]], writes=[rS16[hb]])
                S.op("sp", (lambda e, a=wsc_t(ti)[:, off:off + nel], b=dst16: e.dma_start(out=a, in_=b)),
                     reads=[rS16[hb]], writes=[], key=f"cs{hb}")
                rTILE[ti].r.append(S.ops["sp"][-1][2])
                k += 1
        conv_fence = []
        for r in rS32 + rS16:
            conv_fence += r.wdeps()
        for r in rACT + rHN:
            r.r = list(conv_fence)

        wstate = {"n": 0}

        def wload(name):
            ti = tile_idx[name]
            s = wstate["n"] % NSLOT
            wstate["n"] += 1
            nv = sum(h[1] for h in tiles[ti][1])
            S.op("sp", (lambda e, s=s, ti=ti, nv=nv: e.dma_start(out=RING[s][:, 0:nv], in_=wsc_t(ti)[:, 0:nv])),
                 writes=[rSLOT[s]], extra=list(rTILE[ti].r), key=f"wl{s}")
            return s

        bank_rr = {"n": 0}

        def nbank():
            b = bank_rr["n"] % 8
            bank_rr["n"] += 1
            return b

        def mm_group(b, parts, n, mms, reads):
            nm = len(mms)
            for i, (l, r) in enumerate(mms):
                out = BANK[b][0:parts, 0:n]
                fn = (lambda e, out=out, l=l, r=r, i=i: e.matmul(out, l, r, start=(i == 0), stop=(i == nm - 1)))
                if i == 0 or i == nm - 1:
                    S.op("pe", fn, reads=reads, writes=[rBANK[b]], wait=(i == 0), track=(i == nm - 1))
                else:
                    S.op("pe", fn)

        import os
        STOP = int(os.environ.get("KSTOP", "99"))

        def rms_stats(c0, n, scale_ones, nch, src_fn, rs_i, sq_scale=1.0, sq_bias=EPS):
            for c in range(nch):
                if c % 2 == 0:
                    S.op("act", (lambda e, c=c: e.activation(ACTb[:, c, 0:n], src_fn(c), AF.Square)),
                         reads=[rR[c]], writes=[rACT[c]])
                else:
                    S.op("dve", (lambda e, c=c: e.tensor_tensor(ACTb[:, c, 0:n], src_fn(c), src_fn(c), ALU.mult)),
                         reads=[rR[c]], writes=[rACT[c]])
            b = nbank()
            mm_group(b, 128, n, [(CBF[:, scale_ones, :], ACTb[:, c, 0:n]) for c in range(nch)],
                     reads=[rACT[c] for c in range(nch)])
            S.op("act", (lambda e: e.activation(RS[rs_i][:, 0:n], BANK[b][:, 0:n], AF.Sqrt, bias=sq_bias, scale=sq_scale)),
                 reads=[rBANK[b]], writes=[rRS[rs_i]])
            S.op("dve", (lambda e: e.reciprocal(RS[rs_i][:, 0:n], RS[rs_i][:, 0:n])),
                 reads=[rRS[rs_i]], writes=[rRS[rs_i]])

        def norm_to_hn(c0, n, gcol, rs_i):
            for c in range(DC):
                S.op("dve", (lambda e, c=c: e.scalar_tensor_tensor(
                    HN[:, c, 0:n], R[:, c, c0:c0 + n], GN[:, gcol + c:gcol + c + 1], RS[rs_i][:, 0:n],
                    ALU.mult, ALU.mult)), reads=[rR[c], rRS[rs_i]], writes=[rHN[c]], extra=const_deps)

        def ffn(l, j, c0, n):
            rms_stats(c0, n, 0, DC, lambda c: R[:, c, c0:c0 + n], 0)
            norm_to_hn(c0, n, G_FFN + (l * 2 + j) * 16, 0)
            sgi = 0
            for ftp in range(FC // 2):
                s = wload(("win", l, j, ftp))
                wt = RING[s][:, :].rearrange("p (c g t f) -> p c g t f", c=16, g=2, t=2, f=128)
                for ft in range(2):
                    f = 2 * ftp + ft
                    bg, bu = nbank(), nbank()
                    for (bb, gu) in ((bg, 0), (bu, 1)):
                        mm_group(bb, 128, n, [(wt[:, c, gu, ft, :], HN[:, c, 0:n]) for c in range(DC)],
                                 reads=[rSLOT[s]] + rHN)
                    q = sgi % 2
                    sgi += 1
                    S.op("act", (lambda e, q=q, bg=bg: e.activation(SG[q][:, 0:n], BANK[bg][:, 0:n], AF.Silu)),
                         reads=[rBANK[bg]], writes=[rSG[q]])
                    S.op("dve", (lambda e, q=q, bu=bu, f=f: e.tensor_tensor(
                        ACTb[:, f, 0:n], SG[q][:, 0:n], BANK[bu][:, 0:n], ALU.mult)),
                        reads=[rSG[q], rBANK[bu]], writes=[rACT[f]])
            for dg in range(4):
                bs = [nbank() for _ in range(4)]
                for fq in range(4):
                    s = wload(("wout", l, j, dg, fq))
                    wt = RING[s][:, 0:FA * 512].rearrange("p (a d) -> p a d", a=FA)
                    for a in range(FA):
                        fc = fq * FA + a
                        for dc in range(4):
                            first = (fq == 0 and a == 0)
                            last = (fq == 3 and a == FA - 1)
                            lastslot = (a == FA - 1 and dc == 3)
                            out = BANK[bs[dc]][:, 0:n]
                            fn = (lambda e, out=out, l_=wt[:, a, dc * 128:(dc + 1) * 128], r_=ACTb[:, fc, 0:n],
                                  first=first, last=last: e.matmul(out, l_, r_, start=first, stop=last))
                            h = S.op("pe", fn, reads=[rSLOT[s], rACT[fc]] if (first or dc == 0) else [],
                                     writes=[rBANK[bs[dc]]] if first else [], track=False)
                            if last:
                                rBANK[bs[dc]].w = h
                                rBANK[bs[dc]].r = []
                            if lastslot:
                                rSLOT[s].r.append(h)
                            if dg == 3 and dc == 3:
                                rACT[fc].r.append(h)
                for dc in range(4):
                    c = dg * 4 + dc
                    S.op("dve", (lambda e, c=c, b=bs[dc]: e.scalar_tensor_tensor(
                        R[:, c, c0:c0 + n], BANK[b][:, 0:n], 0.5, R[:, c, c0:c0 + n], ALU.mult, ALU.add)),
                        reads=[rBANK[bs[dc]]], writes=[rR[c]])

        def load_x(i):
            blocks = [(0, 128), (128, 128), (256, 128), (384, W - 384)]
            deps0 = []
            for r in rACT[0:36]:
                deps0 += r.wdeps()
            hx = []
            for tb, (t0, nt) in enumerate(blocks):
                st = ACTf32[:, tb * 2048:(tb + 1) * 2048]
                hx.append(S.op("sp", (lambda e, st=st, t0=t0, nt=nt: e.dma_start(out=st[0:nt, :], in_=xin[i, t0:t0 + nt, :])),
                               extra=deps0, key=f"x{tb}"))
            for r in rACT[0:36]:
                r.w = None
                r.r = []
            for tb, (t0, nt) in enumerate(blocks):
                st = ACTf32[:, tb * 2048:(tb + 1) * 2048]
                for cg in range(4):
                    b = nbank()
                    for kk in range(4):
                        c = cg * 4 + kk
                        fn = (lambda e, b=b, kk=kk, st=st, c=c, nt=nt: e.transpose(
                            BANK[b][:, kk * 128:kk * 128 + nt], st[0:nt, c * 128:(c + 1) * 128], IDN[0:nt, 0:nt]))
                        S.op("pe", fn, writes=[rBANK[b]], extra=const_deps + [hx[tb]], wait=(kk == 0), track=(kk == 3))
                    dst = R[:, cg * 4:cg * 4 + 4, t0:t0 + nt]
                    srcv = BANK[b][:, :].rearrange("p (k t) -> p k t", k=4)[:, :, 0:nt]
                    S.op("dve", (lambda e, dst=dst, srcv=srcv: e.tensor_copy(dst, srcv)),
                         reads=[rBANK[b]], writes=rR[cg * 4:cg * 4 + 4])
            hl = S.ops["pe"][-1][2]
            for r in rACT[0:36]:
                r.r.append(hl)

        def pool_mixer(i):
            n = W
            rms_stats(0, n, 0, DC, lambda c: R[:, c, 0:n], 1)
            HN32 = ACTf32[:, 0:DC * W].rearrange("p (c t) -> p c t", c=DC)
            Y = HN
            for c in range(DC):
                S.op("dve", (lambda e, c=c: e.scalar_tensor_tensor(
                    HN32[:, c, :], R[:, c, 0:n], GN[:, G_MIX + c:G_MIX + c + 1], RS[1][:, 0:n], ALU.mult, ALU.mult)),
                    reads=[rR[c], rRS[1]], writes=[rACT[2 * c], rACT[2 * c + 1]], extra=const_deps)
            s = wload(("pool",))
            wt = RING[s][:, :].rearrange("p (g kc d) -> p g kc d", g=4, kc=4, d=512)
            var = 0 if i == 0 else 1
            for gi in range(4):
                h4 = HN32[:, 4 * gi:4 * gi + 4, :]
                hres = [rACT[x] for x in range(8 * gi, 8 * gi + 8)]
                bufs = [(WA, rWA), (WB, rWB)]
                src, sres = h4, hres
                sh = 1
                lo = 1
                for lev in range(gi + 1):
                    dstb, dres = bufs[lev % 2]
                    S.op("dve", (lambda e, dstb=dstb, src=src, lo=lo, sh=sh: e.tensor_tensor(
                        dstb[:, :, lo:W], src[:, :, lo:W], src[:, :, lo - sh:W - sh], ALU.add)),
                        reads=sres, writes=[dres])
                    src, sres = dstb, [dres]
                    sh *= 2
                    lo = 2 * sh - 1
                for kk in range(4):
                    c = 4 * gi + kk
                    S.op("dve", (lambda e, src=src, kk=kk, gi=gi: e.tensor_tensor(
                        src[:, kk, HALO:W], src[:, kk, HALO:W], INVC[:, var, gi, :], ALU.mult)),
                        reads=[], writes=sres, extra=const_deps)
                    S.op("dve", (lambda e, src=src, kk=kk, c=c: e.tensor_tensor(
                        Y[:, c, 0:T], src[:, kk, HALO:W], HN32[:, c, HALO:W], ALU.subtract)),
                        reads=sres + [rACT[2 * c], rACT[2 * c + 1]], writes=[rHN[c]])
                for dc in range(4):
                    b = nbank()
                    mm_group(b, 128, T, [(wt[:, gi, kc, dc * 128:(dc + 1) * 128], Y[:, 4 * gi + kc, 0:T]) for kc in range(4)],
                             reads=[rSLOT[s]] + rHN[4 * gi:4 * gi + 4])
                    c = 4 * gi + dc
                    S.op("dve", (lambda e, c=c, b=b: e.scalar_tensor_tensor(
                        R[:, c, HALO:W], BANK[b][:, 0:T], GN[:, G_PSC + c:G_PSC + c + 1], R[:, c, HALO:W],
                        ALU.mult, ALU.add)), reads=[rBANK[b]], writes=[rR[c]])

        kv_handles = {}
        cc_handles = []

        def exchange(i):
            if not fused:
                return
            RG = [[2 * q_, 2 * q_ + 1] for q_ in range(nseq)]
            cc_handles.append(S.op("pool", (lambda e: e.collective_compute(
                "AllGather", ALU.bypass, replica_groups=RG,
                ins=[kmine_ts[i].ap().opt()], outs=[kbuf_ts[i].ap().opt()])), extra=kv_handles[i], key="cc", unit=1))
            cc_handles.append(S.op("pool", (lambda e: e.collective_compute(
                "AllGather", ALU.bypass, replica_groups=RG,
                ins=[vmine_ts[i].ap().opt()], outs=[vbuf_ts[i].ap().opt()])), extra=kv_handles[i], key="cc", unit=1))

        def qkv(i):
            n = T
            rms_stats(HALO, n, 0, DC, lambda c: R[:, c, HALO:W], 0)
            norm_to_hn(HALO, n, G_MIX + 16, 0)
            for which, gcol, base, sq_scale in (("q", G_QN, 0, 128.0), ("k", G_KN, 16, 1.0)):
                for hq in range(4):
                    s = wload((which, hq))
                    wt = RING[s][:, :].rearrange("p (c f) -> p c f", c=16)
                    for hh in range(4):
                        h = hq * 4 + hh
                        b = nbank()
                        mm_group(b, 128, n, [(wt[:, c, hh * 128:(hh + 1) * 128], HN[:, c, 0:n]) for c in range(DC)],
                                 reads=[rSLOT[s]] + rHN)
                        qq = h % 2
                        tmp = 42 + (h % 2)
                        S.op("dve", (lambda e, b=b, qq=qq: e.tensor_copy(SG[qq][:, 0:n], BANK[b][:, 0:n])),
                             reads=[rBANK[b]], writes=[rSG[qq]])
                        S.op("act", (lambda e, qq=qq, tmp=tmp: e.activation(ACTb[:, tmp, 0:n], SG[qq][:, 0:n], AF.Square)),
                             reads=[rSG[qq]], writes=[rACT[tmp]])
                        b2 = nbank()
                        mm_group(b2, 128, n, [(CBF[:, 1, :], ACTb[:, tmp, 0:n])], reads=[rACT[tmp]])
                        S.op("act", (lambda e, b2=b2, qq=qq, sq_scale=sq_scale: e.activation(
                            RS[qq][:, 0:n], BANK[b2][:, 0:n], AF.Sqrt, bias=EPS * sq_scale, scale=sq_scale)),
                            reads=[rBANK[b2]], writes=[rRS[qq]])
                        S.op("dve", (lambda e, qq=qq: e.reciprocal(RS[qq][:, 0:n], RS[qq][:, 0:n])),
                             reads=[rRS[qq]], writes=[rRS[qq]])
                        S.op("dve", (lambda e, qq=qq, h=h, base=base, gcol=gcol: e.scalar_tensor_tensor(
                            ACTb[:, base + h, 0:n], SG[qq][:, 0:n], GN[:, gcol:gcol + 1], RS[qq][:, 0:n],
                            ALU.mult, ALU.mult)), reads=[rSG[qq], rRS[qq]], writes=[rACT[base + h]], extra=const_deps)
            if STOP < 7:
                return
            S.op("sp", (lambda e: e.dma_start(out=qs_w[i].rearrange("p (h t) -> p h t", h=16), in_=ACTb[:, 0:16, 0:T])),
                 reads=rACT[0:16], key="qst")
            S.op("sp", (lambda e: e.dma_start(out=kmine_ap(i).rearrange("p (h t) -> p h t", h=16), in_=ACTb[:, 16:32, 0:T])),
                 reads=rACT[16:32], key="kst")
            kv_handles[i] = [S.ops["sp"][-1][2]]
            if STOP < 8:
                return
            for hq in range(4):
                s = wload(("v", hq))
                wt = RING[s][:, :].rearrange("p (c f) -> p c f", c=16)
                vq = hq % 2
                c_lo = 32 + 5 * vq
                vres = rACT[c_lo:c_lo + 5]
                VQ = ACTflat[:, c_lo * W:c_lo * W + 2048].rearrange("p (h kb f) -> p h kb f", h=4, kb=4, f=128)
                hs = []
                for kb in range(NKB):
                    b = nbank()
                    mm_group(b, KB, 512, [(HN[:, c, kb * KB:(kb + 1) * KB], wt[:, c, :]) for c in range(DC)],
                             reads=[rSLOT[s]] + rHN)
                    dst = VQ[0:KB, :, kb, :]
                    srcv = BANK[b][0:KB, :].rearrange("p (h f) -> p h f", h=4)
                    if kb % 2 == 0:
                        hs.append(S.op("dve", (lambda e, dst=dst, srcv=srcv: e.tensor_copy(dst, srcv)),
                                       reads=[rBANK[b]], writes=vres, track=False))
                    else:
                        hs.append(S.op("act", (lambda e, dst=dst, srcv=srcv: e.activation(dst, srcv, AF.Copy)),
                                       reads=[rBANK[b]], writes=vres, track=False))
                    rBANK[b].r.append(hs[-1])
                if STOP < 9:
                    continue
                hv = S.op("sp", (lambda e, hq=hq, c_lo=c_lo: e.dma_start(
                    out=vmine_ap(i)[:, hq * 2048:(hq + 1) * 2048], in_=ACTflat[0:KB, c_lo * W:c_lo * W + 2048])),
                    extra=hs, key=f"vst{vq}")
                for r in vres:
                    r.w = None
                    r.r = [hv]
                kv_handles[i].append(hv)


        def phaseA_tile(i):
            if STOP >= 2:
                load_x(i)
            if STOP >= 3:
                ffn(0, 0, 0, W)
            if i > 0:
                exchange(i - 1)
            if STOP >= 4:
                pool_mixer(i)
            if STOP >= 5:
                ffn(0, 1, HALO, T)
                ffn(1, 0, HALO, T)
            S.op("sp", (lambda e: e.dma_start(out=h1s_w[i].rearrange("p (c t) -> p c t", c=DC), in_=R[:, :, HALO:W])),
                 reads=rR, key="h1st")
            if STOP >= 6:
                qkv(i)

        rKT = [Res() for _ in range(4)]
        rV = [Res() for _ in range(4)]
        rE = [Res() for _ in range(3)]
        rSP = [Res() for _ in range(3)]
        rS = [Res() for _ in range(3)]
        rEX = [Res() for _ in range(3)]
        rA = [Res() for _ in range(4)]
        ZB, GB, OB = (0, 1, 2), (3, 4, 7), (5, 6)
        cnt = {"kv": 0, "blk": 0, "S": 0, "o": 0}

        def attention(i, extra_in):
            QT = HN
            OT = ACTb
            nch = 2 * i + 2
            blocks = []
            for h in range(16):
                for jj in range(nch - 1, -1, -1):
                    for kb in range(NKB - 1, -1, -1):
                        blocks.append((h, jj, kb))
            nb = len(blocks)
            st = {}

            nvis = nb // NKB

            def issue_kv(vv):
                h, jj, _ = blocks[vv * NKB]
                q = cnt["kv"] % 4
                cnt["kv"] += 1
                st[("kv", h, jj)] = q
                S.op("sp", (lambda e, q=q, h=h, jj=jj: e.dma_start(
                    out=KTb[q][:, :], in_=kbuf_ap(jj).rearrange("p (h t) -> p h t", h=16)[:, h, :])),
                    writes=[rKT[q]], extra=extra_in, key=f"kl{q}")
                S.op("sp", (lambda e, q=q, h=h, jj=jj: e.dma_start(
                    out=Vb[q][:, :, :], in_=vbuf_ap(jj).rearrange("p (h kb f) -> p h kb f", h=16, kb=4)[:, h, :, :])),
                    writes=[rV[q]], extra=extra_in, key=f"vl{q}")

            def stage_z(bi):
                h, jj, kb = blocks[bi]
                if kb == NKB - 1:
                    v0 = bi // NKB
                    for vv in ([0, 1, 2] if v0 == 0 else [v0 + 2]):
                        if vv < nvis:
                            issue_kv(vv)
                q = st[("kv", h, jj)]
                zb = ZB[bi % 3]
                S.op("pe", (lambda e, zb=zb, q=q, kb=kb, h=h: e.matmul(
                    BANK[zb][0:KB, 0:T], KTb[q][:, kb * KB:(kb + 1) * KB], QT[:, h, 0:T], start=True, stop=True)),
                    reads=[rKT[q], rHN[h]], writes=[rBANK[zb]])
                ei = bi % 3
                S.op("act", (lambda e, ei=ei, zb=zb: e.activation(Eb[ei][:, :], BANK[zb][0:KB, 0:T], AF.Exp)),
                     reads=[rBANK[zb]], writes=[rE[ei]])
                if jj >= nch - 2:
                    slot = 0 if jj == nch - 1 else 1
                    S.op("pool", (lambda e, ei=ei, slot=slot, kb=kb: e.tensor_tensor(
                        Eb[ei][:, :], Eb[ei][:, :], MASK[:, slot * NKB + kb, :], ALU.mult)),
                        reads=[], writes=[rE[ei]], extra=const_deps)
                S.op("act", (lambda e, ei=ei: e.activation(SPb[ei][:, :], Eb[ei][:, :], AF.Ln, bias=1.0, scale=1.0)),
                     reads=[rE[ei]], writes=[rSP[ei]])

            def stage_g(bi):
                h, jj, kb = blocks[bi]
                first = (jj == nch - 1 and kb == NKB - 1)
                ei = bi % 3
                gb = GB[bi % 3]
                if first:
                    S.op("pe", (lambda e, gb=gb, ei=ei: e.matmul(
                        BANK[gb][0:KB, 0:T], CBF[0:KB, 2, 0:KB], SPb[ei][:, :], start=True, stop=True)),
                        reads=[rSP[ei]], writes=[rBANK[gb]], extra=const_deps)
                    si = cnt["S"] % 3
                    cnt["S"] += 1
                    S.op("pool", (lambda e, si=si, ei=ei: e.tensor_copy(Sb[si][:, :], SPb[ei][:, :])),
                         reads=[rSP[ei]], writes=[rS[si]])
                else:
                    sprev = (cnt["S"] - 1) % 3
                    S.op("pe", (lambda e, gb=gb, ei=ei: e.matmul(
                        BANK[gb][0:KB, 0:T], CBF[0:KB, 2, 0:KB], SPb[ei][:, :], start=True, stop=False)),
                        reads=[rSP[ei]], writes=[rBANK[gb]], extra=const_deps)
                    S.op("pe", (lambda e, gb=gb, sprev=sprev: e.matmul(
                        BANK[gb][0:KB, 0:T], CBF[0:KB, 3, 0:KB], Sb[sprev][:, :], start=False, stop=True)),
                        reads=[rS[sprev]], writes=[rBANK[gb]])
                    last = (jj == 0 and kb == 0)
                    if not last:
                        si = cnt["S"] % 3
                        cnt["S"] += 1
                        S.op("pool", (lambda e, si=si, sprev=sprev, ei=ei: e.tensor_tensor(
                            Sb[si][:, :], Sb[sprev][:, :], SPb[ei][:, :], ALU.add)),
                            reads=[rSP[ei], rS[sprev]], writes=[rS[si]])

            def stage_a(bi):
                ei = bi % 3
                gb = GB[bi % 3]
                xi = bi % 3
                ai = bi % 4
                S.op("act", (lambda e, xi=xi, gb=gb: e.activation(EXb[xi][:, :], BANK[gb][0:KB, 0:T], AF.Exp)),
                     reads=[rBANK[gb]], writes=[rEX[xi]])
                S.op("dve", (lambda e, ei=ei, xi=xi, ai=ai: e.tensor_tensor(Ab[ai][:, :], Eb[ei][:, :], EXb[xi][:, :], ALU.mult)),
                     reads=[rE[ei], rEX[xi]], writes=[rA[ai]])

            def stage_o(bi):
                h, jj, kb = blocks[bi]
                first = (jj == nch - 1 and kb == NKB - 1)
                last = (jj == 0 and kb == 0)
                ai = bi % 4
                q = st[("kv", h, jj)]
                ob = OB[h % 2]
                S.op("pe", (lambda e, ob=ob, q=q, kb=kb, ai=ai, first=first, last=last: e.matmul(
                    BANK[ob][:, 0:T], Vb[q][:, kb, :], Ab[ai][:, :], start=first, stop=last)),
                    reads=[rV[q], rA[ai]], writes=[rBANK[ob]] if (first or last) else [])
                if last:
                    S.op("dve", (lambda e, ob=ob, h=h: e.tensor_copy(OT[:, h, 0:T], BANK[ob][:, 0:T])),
                         reads=[rBANK[ob]], writes=[rACT[h]])

            for kk in range(nb + 4):
                if kk < nb:
                    stage_z(kk)
                if 0 <= kk - 1 < nb:
                    stage_g(kk - 1)
                if 0 <= kk - 2 < nb:
                    stage_a(kk - 2)
                if 0 <= kk - 4 < nb:
                    stage_o(kk - 4)

        def oproj():
            for dg in range(4):
                s = wload(("o", dg))
                wt = RING[s][:, :].rearrange("p (c f) -> p c f", c=16)
                for dc in range(4):
                    b = nbank()
                    mm_group(b, 128, T, [(wt[:, hc, dc * 128:(dc + 1) * 128], ACTb[:, hc, 0:T]) for hc in range(16)],
                             reads=[rSLOT[s]] + rACT[0:16])
                    c = dg * 4 + dc
                    S.op("dve", (lambda e, c=c, b=b: e.tensor_tensor(
                        R[:, c, HALO:W], R[:, c, HALO:W], BANK[b][:, 0:T], ALU.add)),
                        reads=[rBANK[b]], writes=[rR[c]])

        def store_out(i):
            OS = ACTf32[:, 0:8192].rearrange("p (kb d) -> p kb d", kb=4)
            ocopies = []
            for kb in range(NKB):
                for cg in range(4):
                    b = nbank()
                    for kk in range(4):
                        c = cg * 4 + kk
                        fn = (lambda e, b=b, kk=kk, c=c, kb=kb: e.transpose(
                            BANK[b][0:KB, kk * 128:(kk + 1) * 128], R[:, c, HALO + kb * KB:HALO + (kb + 1) * KB], IDN[:, :]))
                        S.op("pe", fn, reads=[rR[c]], writes=[rBANK[b]], extra=const_deps)
                    osv = OS[0:KB, kb, cg * 512:(cg + 1) * 512]
                    if cg % 2 == 0:
                        hc = S.op("dve", (lambda e, b=b, osv=osv: e.tensor_copy(osv, BANK[b][0:KB, :])),
                                  reads=[rBANK[b]], writes=rACT[0:36], track=False)
                    else:
                        hc = S.op("act", (lambda e, b=b, osv=osv: e.activation(osv, BANK[b][0:KB, :], AF.Copy)),
                                  reads=[rBANK[b]], writes=rACT[0:36], track=False)
                    rBANK[b].r.append(hc)
                    ocopies.append(hc)
            S.op("sp", (lambda e: e.dma_start(out=yout[i].rearrange("(kb p) d -> p kb d", p=KB), in_=OS[0:KB, :, :])),
                 extra=ocopies, key="yst")
            hy = S.ops["sp"][-1][2]
            for x in range(0, 36):
                rACT[x].w = None
                rACT[x].r = [hy]
            return hy

        def phaseB_tile(i, extra_in):
            S.op("sp", (lambda e: e.dma_start(out=R[:, :, HALO:W], in_=h1s_r[i].rearrange("p (c t) -> p c t", c=DC))),
                 writes=rR, extra=extra_in, key="h1ld")
            S.op("sp", (lambda e: e.dma_start(out=HN[:, :, 0:T], in_=qs_r[i].rearrange("p (h t) -> p h t", h=16))),
                 writes=rHN, extra=extra_in, key="qld")
            attention(i, extra_in)
            oproj()
            ffn(1, 1, HALO, T)
            return store_out(i)

        a_done = []
        if doA:
            for i in range(NT):
                phaseA_tile(i)
            a_done = [h for (_, _, h) in S.ops["sp"] if h.key in ("h1st", "qst", "kst", "vst0", "vst1")]
        last_out = None
        if doB:
            extra_in = []
            if fused:
                exchange(NT - 1)
                extra_in = [cc_handles[-1]] + a_done
            for i in range(NT):
                last_out = phaseB_tile(i, extra_in)
        finals = a_done if not doB else [h for (_, _, h) in S.ops["sp"] if h.key == "yst"]
        S.ops["sp"].append(((lambda e: _Nop()), finals, Hd("sp", len(S.ops["sp"]))))

        keys = sorted(S.dma_count.keys())
        sems = {e: es.enter_context(nc.semaphore(f"s_{e}")) for e in Sched.ENGS}
        dsems = {k: es.enter_context(nc.semaphore(f"d_{k}")) for k in keys}
        block = es.enter_context(nc.Block())
        S.emit(nc, block, sems, dsems)
    return nc


class _Nop:
    def then_inc(self, *a, **k):
        return self


def _consts():
    ident = np.eye(128, dtype=np.float32)
    cbf = np.zeros((128, 4, 128), dtype=np.float32)
    cbf[:, 0, :] = 1.0 / D
    cbf[:, 1, :] = 1.0 / 128
    j = np.arange(128)[:, None]
    m = np.arange(128)[None, :]
    cbf[:, 2, :] = -(j >= m).astype(np.float32)
    cbf[:, 3, :] = -1.0
    return ident, cbf.astype(ml_dtypes.bfloat16)


def _fm(v):
    return np.ascontiguousarray(np.asarray(v, dtype=np.float32).reshape(DC, 128).T)


def kernel(x, meta, ffn_norm, ffn_w_in, ffn_w_out, mix_norm, pool_w, pool_scale, sb_w_qkv, sb_qk_norm, sb_w_o):
    x = np.asarray(x, dtype=np.float32)
    B = x.shape[0]
    ident, cbf = _consts()
    gains = np.zeros((128, NG), dtype=np.float32)
    for l in range(2):
        for j in range(2):
            gains[:, G_FFN + (l * 2 + j) * 16: G_FFN + (l * 2 + j) * 16 + 16] = _fm(ffn_norm[l, j])
        gains[:, G_MIX + l * 16: G_MIX + l * 16 + 16] = _fm(mix_norm[l])
    gains[:, G_PSC:G_PSC + 16] = _fm(pool_scale[0])
    gains[:, G_QN] = np.asarray(sb_qk_norm[0, 0], dtype=np.float32)
    gains[:, G_KN] = np.asarray(sb_qk_norm[0, 1], dtype=np.float32)

    invc = np.zeros((2, 128, 4, T), dtype=np.float32)
    pos = np.arange(T)
    for gi, w in enumerate((2, 4, 8, 16)):
        invc[0, :, gi, :] = 1.0 / np.minimum(pos + 1, w)
        invc[1, :, gi, :] = 1.0 / w
    s_idx = np.arange(KB)[:, None]
    t_idx = np.arange(T)[None, :]
    diag = np.stack([((kb * KB + s_idx) < t_idx) for kb in range(NKB)], axis=1).astype(np.float32)
    masks = []
    for p in range(2):
        m = np.zeros((KB, 2 * NKB, T), dtype=np.float32)
        if p == 0:
            m[:, NKB:, :] = diag
        else:
            m[:, :NKB, :] = diag
            m[:, NKB:, :] = 1.0
        masks.append(m.astype(ml_dtypes.bfloat16))

    ncores = 2 * B
    seq = x.shape[1]
    assert NMETA + seq == 2 * NT * T
    xins = []
    for c in range(ncores):
        s, p = c // 2, c % 2
        full = np.concatenate([np.zeros((HALO, D), np.float32), np.asarray(meta, np.float32), x[s]], axis=0)
        xi = np.empty((NT, W, D), dtype=np.float32)
        for i in range(NT):
            g = 2 * i + p
            xi[i] = full[T * g: T * g + W]
        xins.append(xi)

    w_in_np = np.asarray(ffn_w_in, dtype=np.float32)
    w_out_np = np.asarray(ffn_w_out, dtype=np.float32)
    common = {"ffn_w_in": w_in_np, "ffn_w_out": w_out_np, "gains": gains, "ident": ident, "cbf": cbf}

    if FUSED:
        ncF = build_program("AB", nseq=B)
        mapsF = []
        for c in range(ncores):
            p = c % 2
            m = dict(common)
            invc_c = invc if (p == 0) else np.stack([invc[1], invc[1]], axis=0)
            m.update({"xin": xins[c], "invc": invc_c, "pool_w": np.asarray(pool_w[0], np.float32),
                      "sb_w_qkv": np.asarray(sb_w_qkv[0], np.float32),
                      "sb_w_o": np.asarray(sb_w_o[0], np.float32), "mask": masks[p]})
            mapsF.append(m)
        resB = run_bass_kernel_spmd(ncF, mapsF, core_ids=list(range(ncores))).results
    else:
        ncA = build_program("A")
        mapsA = []
        for c in range(ncores):
            m = dict(common)
            invc_c = invc if (c % 2 == 0) else np.stack([invc[1], invc[1]], axis=0)
            m.update({"xin": xins[c], "invc": invc_c, "pool_w": np.asarray(pool_w[0], np.float32),
                      "sb_w_qkv": np.asarray(sb_w_qkv[0], np.float32)})
            mapsA.append(m)
        resA = run_bass_kernel_spmd(ncA, mapsA, core_ids=list(range(ncores))).results

        ncB = build_program("B")
        mapsB = []
        for c in range(ncores):
            s, p = c // 2, c % 2
            m = dict(common)
            kb_ = np.concatenate([resA[2 * s]["kmine"], resA[2 * s + 1]["kmine"]], axis=0)
            vb_ = np.concatenate([resA[2 * s]["vmine"], resA[2 * s + 1]["vmine"]], axis=0)
            m.update({"sb_w_o": np.asarray(sb_w_o[0], np.float32), "mask": masks[p],
                      "h1s": resA[c]["h1s"], "qs": resA[c]["qs"], "kbuf": kb_, "vbuf": vb_})
            mapsB.append(m)
        resB = run_bass_kernel_spmd(ncB, mapsB, core_ids=list(range(ncores))).results

    out = np.empty((B, seq, D), dtype=np.float32)
    for c in range(ncores):
        s, p = c // 2, c % 2
        y = resB[c]["y"]
        for i in range(NT):
            g = 2 * i + p
            p0 = T * g - NMETA
            if p0 < 0:
                out[s, 0:T - NMETA] = y[i, NMETA:]
            else:
                out[s, p0:p0 + T] = y[i]
    return out
```
